# Optimizing a Trainium2 kernel written in Bass

```python
import math
import jax, jax.numpy as jnp
from jax import lax
import numpy as np

D_MODEL = 1024
BATCH = 8
SEQ = 8192
DEPTH = 1

EXPAND = 2
D_INNER = EXPAND * D_MODEL
D_SSM = D_INNER // 2
D_ATTN = D_INNER - D_SSM
SSM_HEAD_DIM = 64
SSM_HEADS = D_SSM // SSM_HEAD_DIM
SSM_GROUPS = 2
SSM_HPG = SSM_HEADS // SSM_GROUPS
SSM_STATE = 128
CONV_K = 4
CHUNK = 128
CONV_DIM = D_SSM + 2 * SSM_GROUPS * SSM_STATE
ATTN_HEAD_DIM = 64
ATTN_HEADS = D_ATTN // ATTN_HEAD_DIM
Q_BLOCK = 128
EPS = 1e-6

N_IN = (2 * D_SSM + 2 * SSM_GROUPS * SSM_STATE + SSM_HEADS) + (4 * D_ATTN + ATTN_HEADS)
SPLITS = np.cumsum([D_SSM, D_SSM, SSM_GROUPS * SSM_STATE, SSM_GROUPS * SSM_STATE, SSM_HEADS,
                    D_ATTN, D_ATTN, D_ATTN, D_ATTN]).tolist()

kernel_name = "hymba_ssd_fox_adaln_block"


def rms_norm(x, gain):
    xf = x.astype(jnp.float32)
    out = xf * lax.rsqrt(jnp.mean(xf * xf, axis=-1, keepdims=True) + EPS)
    return (out * gain.astype(jnp.float32)).astype(x.dtype)


def causal_depthwise_conv(u, w, b):
    out = lax.conv_general_dilated(u, w[:, None, :].astype(u.dtype), window_strides=(1,),
                                   padding=[(CONV_K - 1, 0)],
                                   dimension_numbers=("NWC", "WIO", "NWC"),
                                   feature_group_count=u.shape[-1])
    return out + b.astype(u.dtype)


def ssd_chunked(xs, dt, a, bm, cm):
    bsz, seq = xs.shape[:2]
    nc = seq // CHUNK
    xs = xs.reshape(bsz, nc, CHUNK, SSM_GROUPS, SSM_HPG, SSM_HEAD_DIM).astype(jnp.float32)
    dt = dt.reshape(bsz, nc, CHUNK, SSM_GROUPS, SSM_HPG)
    bm = bm.reshape(bsz, nc, CHUNK, SSM_GROUPS, SSM_STATE).astype(jnp.float32)
    cm = cm.reshape(bsz, nc, CHUNK, SSM_GROUPS, SSM_STATE).astype(jnp.float32)
    cum = jnp.cumsum(dt * a.reshape(SSM_GROUPS, SSM_HPG), axis=2)
    xdt = xs * dt[..., None]
    causal = jnp.tril(jnp.ones((CHUNK, CHUNK), dtype=bool))
    seg = cum[:, :, :, None] - cum[:, :, None, :]
    decay_in = jnp.exp(jnp.where(causal[None, None, :, :, None, None], seg, -jnp.inf))
    cb = jnp.einsum('bclgn,bcsgn->bclsg', cm, bm)
    y_diag = jnp.einsum('bclsgr,bcsgrp->bclgrp', cb[..., None] * decay_in, xdt)
    decay_to_end = jnp.exp(cum[:, :, -1:] - cum)
    states = jnp.einsum('bclgn,bclgr,bclgrp->bcgrpn', bm, decay_to_end, xdt)
    chunk_decay = jnp.exp(cum[:, :, -1])

    def step(h, inp):
        st, dec = inp
        return h * dec[..., None, None] + st, h

    h0 = jnp.zeros((bsz, SSM_GROUPS, SSM_HPG, SSM_HEAD_DIM, SSM_STATE), jnp.float32)
    _, prev = lax.scan(step, h0, (jnp.moveaxis(states, 1, 0), jnp.moveaxis(chunk_decay, 1, 0)))
    prev = jnp.moveaxis(prev, 0, 1)
    y_off = jnp.einsum('bclgn,bcgrpn,bclgr->bclgrp', cm, prev, jnp.exp(cum))
    return (y_diag + y_off).reshape(bsz, seq, SSM_HEADS, SSM_HEAD_DIM)


def forgetting_attention(q, k, v, log_f):
    seq = q.shape[1]
    scale = 1.0 / math.sqrt(ATTN_HEAD_DIM)
    fcum = jnp.transpose(jnp.cumsum(log_f, axis=1), (0, 2, 1))
    outs = []
    for i in range(seq // Q_BLOCK):
        qs, qe = i * Q_BLOCK, (i + 1) * Q_BLOCK
        s = jnp.einsum('bqhd,bkhd->bhqk', q[:, qs:qe], k[:, :qe]).astype(jnp.float32) * scale
        s = s + fcum[:, :, qs:qe, None] - fcum[:, :, None, :qe]
        mask = jnp.arange(qe)[None, :] <= (qs + jnp.arange(Q_BLOCK))[:, None]
        s = jnp.where(mask, s, -jnp.inf)
        p = jax.nn.softmax(s, axis=-1).astype(v.dtype)
        outs.append(jnp.einsum('bhqk,bkhd->bqhd', p, v[:, :qe]))
    return jnp.concatenate(outs, axis=1)


def hybrid_layer(x, c, norm_gain, w_ada, b_ada, w_in, conv_w, conv_b, dt_bias, a_log, d_skip,
                 ssm_norm_gain, q_norm_gain, k_norm_gain, forget_bias, attn_norm_gain, w_out):
    bsz, seq, _ = x.shape
    mod = jax.nn.silu(c) @ w_ada + b_ada
    shift, scale, gate = jnp.split(mod, 3, axis=-1)
    h = rms_norm(x, norm_gain) * (1 + scale[:, None, :]) + shift[:, None, :]

    proj = h @ w_in
    z_ssd, x_ssd, b_ssd, c_ssd, dt_raw, q, k, v, z_attn, f_raw = jnp.split(proj, SPLITS, axis=-1)

    xbc = jax.nn.silu(causal_depthwise_conv(jnp.concatenate([x_ssd, b_ssd, c_ssd], -1), conv_w, conv_b))
    x_c, b_c, c_c = jnp.split(xbc, [D_SSM, D_SSM + SSM_GROUPS * SSM_STATE], axis=-1)
    dt = jax.nn.softplus(dt_raw.astype(jnp.float32) + dt_bias.astype(jnp.float32))
    a = -jnp.exp(a_log.astype(jnp.float32))
    xh = x_c.reshape(bsz, seq, SSM_HEADS, SSM_HEAD_DIM)
    y = ssd_chunked(xh, dt, a,
                    b_c.reshape(bsz, seq, SSM_GROUPS, SSM_STATE),
                    c_c.reshape(bsz, seq, SSM_GROUPS, SSM_STATE))
    y = (y + xh.astype(jnp.float32) * d_skip.astype(jnp.float32)[:, None]).astype(x.dtype)
    y = y.reshape(bsz, seq, D_SSM) * jax.nn.silu(z_ssd)
    y = rms_norm(y.reshape(bsz, seq, SSM_GROUPS, D_SSM // SSM_GROUPS),
                 ssm_norm_gain.reshape(SSM_GROUPS, -1)).reshape(bsz, seq, D_SSM)

    hs = (bsz, seq, ATTN_HEADS, ATTN_HEAD_DIM)
    qh = rms_norm(q.reshape(hs), q_norm_gain)
    kh = rms_norm(k.reshape(hs), k_norm_gain)
    log_f = jax.nn.log_sigmoid(f_raw.astype(jnp.float32) + forget_bias.astype(jnp.float32))
    o = forgetting_attention(qh, kh, v.reshape(hs), log_f)
    o = rms_norm(o, attn_norm_gain.reshape(ATTN_HEADS, ATTN_HEAD_DIM)).reshape(bsz, seq, D_ATTN)
    o = o * jax.nn.silu(z_attn)

    mixed = jnp.concatenate([y, o], axis=-1) @ w_out
    return x + gate[:, None, :] * mixed


def setup_inputs(seed: int = 0) -> dict:
    key = jax.random.key(seed)
    ks = jax.random.split(key, 20)
    f32 = jnp.float32
    nrm = lambda k, shape, s: jax.random.normal(k, shape, f32) * s
    dt0 = jnp.exp(jax.random.uniform(ks[8], (DEPTH, SSM_HEADS), f32, math.log(1e-3), math.log(1e-1)))
    return {
        "x": jax.random.normal(ks[0], (BATCH, SEQ, D_MODEL), f32),
        "c": jax.random.normal(ks[1], (BATCH, D_MODEL), f32),
        "norm_gain": 1.0 + nrm(ks[2], (DEPTH, D_MODEL), 0.02),
        "w_ada": nrm(ks[3], (DEPTH, D_MODEL, 3 * D_MODEL), D_MODEL ** -0.5 * 0.5),
        "b_ada": nrm(ks[4], (DEPTH, 3 * D_MODEL), 0.02),
        "w_in": nrm(ks[5], (DEPTH, D_MODEL, N_IN), D_MODEL ** -0.5),
        "conv_w": nrm(ks[6], (DEPTH, CONV_K, CONV_DIM), CONV_K ** -0.5),
        "conv_b": nrm(ks[7], (DEPTH, CONV_DIM), 0.02),
        "dt_bias": dt0 + jnp.log(-jnp.expm1(-dt0)),
        "a_log": jnp.log(jax.random.uniform(ks[9], (DEPTH, SSM_HEADS), f32, 1.0, 16.0)),
        "d_skip": 1.0 + nrm(ks[10], (DEPTH, SSM_HEADS), 0.02),
        "ssm_norm_gain": 1.0 + nrm(ks[11], (DEPTH, D_SSM), 0.02),
        "q_norm_gain": 1.0 + nrm(ks[12], (DEPTH, ATTN_HEAD_DIM), 0.02),
        "k_norm_gain": 1.0 + nrm(ks[13], (DEPTH, ATTN_HEAD_DIM), 0.02),
        "forget_bias": jax.random.uniform(ks[14], (DEPTH, ATTN_HEADS), f32, 1.0, 5.0),
        "attn_norm_gain": 1.0 + nrm(ks[15], (DEPTH, D_ATTN), 0.02),
        "w_out": nrm(ks[16], (DEPTH, D_INNER, D_MODEL), D_INNER ** -0.5),
    }


def reference(x, c, norm_gain, w_ada, b_ada, w_in, conv_w, conv_b, dt_bias, a_log, d_skip,
              ssm_norm_gain, q_norm_gain, k_norm_gain, forget_bias, attn_norm_gain, w_out):
    for layer in range(DEPTH):
        x = hybrid_layer(x, c, norm_gain[layer], w_ada[layer], b_ada[layer], w_in[layer],
                         conv_w[layer], conv_b[layer], dt_bias[layer], a_log[layer], d_skip[layer],
                         ssm_norm_gain[layer], q_norm_gain[layer], k_norm_gain[layer],
                         forget_bias[layer], attn_norm_gain[layer], w_out[layer])
    return x
```

```python
import numpy as np
import concourse.bass as bass
import concourse.mybir as mybir
from concourse.bass_utils import run_bass_kernel_spmd

F32 = mybir.dt.float32
BF16 = mybir.dt.bfloat16
AF = mybir.ActivationFunctionType
ALU = mybir.AluOpType
AX = mybir.AxisListType

D = 1024
NH = 16
EPS = 1e-6
SEQ = 8192
NCORES = 8


class FW:
    def __init__(self, nc):
        self.nc = nc
        self.eng = {"pe": nc.tensor, "act": nc.scalar, "dve": nc.vector, "pool": nc.gpsimd, "sp": nc.sync}
        self.esem = {k: nc.semaphore("sem_" + k).__enter__() for k in self.eng}
        self.tick = {k: 0 for k in self.eng}
        self.waited = {}
        self.last_w = {}
        self.readers = {}
        self.dma_sems = {}
        self.n_inst = 0
        self.n_wait = 0

    def _wait(self, e, sig):
        sem, val, src = sig
        if src == e and e in ("pe", "sp"):
            return
        k = (e, id(sem))
        if self.waited.get(k, 0) >= val:
            return
        self.waited[k] = val
        self.eng[e].wait_ge(sem, val)
        self.n_wait += 1

    def _deps(self, e, reads, writes):
        for k in reads:
            s = self.last_w.get(k)
            if s is not None:
                self._wait(e, s)
        for k in writes:
            s = self.last_w.get(k)
            if s is not None:
                self._wait(e, s)
            for r in self.readers.get(k, ()):
                self._wait(e, r)

    def _commit(self, sig, reads, writes):
        for k in writes:
            self.last_w[k] = sig
            self.readers[k] = []
        for k in reads:
            if k in writes:
                continue
            lst = self.readers.setdefault(k, [])
            lst[:] = [r for r in lst if r[0] is not sig[0]]
            lst.append(sig)

    def op(self, e, fn, reads=(), writes=()):
        self._deps(e, reads, writes)
        ins = fn()
        self.tick[e] += 1
        ins.then_inc(self.esem[e], 1)
        self._commit((self.esem[e], self.tick[e], e), reads, writes)
        self.n_inst += 1
        return ins

    def dma(self, e, semkey, out, in_, reads=(), writes=()):
        self._deps(e, reads, writes)
        if semkey not in self.dma_sems:
            self.dma_sems[semkey] = [self.nc.semaphore("dsem%d" % len(self.dma_sems)).__enter__(), 0]
        ent = self.dma_sems[semkey]
        ins = self.eng[e].dma_start(out=out, in_=in_)
        ent[1] += 16
        ins.then_inc(ent[0], 16)
        self._commit((ent[0], ent[1], "dma"), reads, writes)
        self.n_inst += 1
        return ins

    def barrier(self):
        for e in self.eng:
            for k, ent in self.dma_sems.items():
                if ent[1] > 0:
                    self._wait(e, (ent[0], ent[1], "dma"))
            for k in self.eng:
                if self.tick[k] > 0 and k != e:
                    self._wait(e, (self.esem[k], self.tick[k], k))
        self.last_w.clear()
        self.readers.clear()


def build_nc(L, debug=False):
    NB = L // 128
    NT = L // 512
    nc = bass.Bass("TRN2", target_bir_lowering=False)
    fw = FW(nc)

    def din(name, shape, dt=F32):
        return nc.dram_tensor(name, list(shape), dt, kind="ExternalInput").ap()

    def dscr(name, shape, dt):
        return nc.dram_tensor(name, list(shape), dt, kind=("ExternalOutput" if debug else "Internal")).ap()

    x_d = din("x", [L, D])
    c_fm_d = din("c_fm", [128, 8])
    ng_fm_d = din("ng_fm", [128, 8])
    wada_d = din("w_ada", [D, 3 * D])
    bada_fm_d = din("bada_fm", [128, 24])
    bgate_bc_d = din("bgate_bc", [128, D])
    win_d = din("w_in", [D, 6688])
    convw_bc_d = din("convw_bc", [128, 4 * 1536])
    convb_row_d = din("convb_row", [1, 1536])
    dtb_bc_d = din("dtb_bc", [128, NH])
    alog_bc_d = din("alog_bc", [128, NH])
    dskip_bc_d = din("dskip_bc", [128, NH])
    fb_bc_d = din("fb_bc", [128, NH])
    ssmg_fm_d = din("ssmg_fm", [128, 8])
    attng_fm_d = din("attng_fm", [128, 8])
    gq2_d = din("gq2", [128, 1])
    gk2_d = din("gk2", [128, 1])
    wout_d = din("w_out", [2 * D, D])
    out_d = nc.dram_tensor("out", [L, D], F32, kind="ExternalOutput").ap()

    yT_s = dscr("yT_s", [D, L], BF16)
    oT_s = dscr("oT_s", [D, L], BF16)
    zsT_s = dscr("zsT_s", [D, L], BF16)
    QT_s = dscr("QT_s", [D, L], BF16)
    KT_s = dscr("KT_s", [D, L], BF16)
    QA_s = dscr("QA_s", [NH, L], BF16)
    V_s = dscr("V_s", [L, NH, 65], BF16)

    import contextlib
    es_all = contextlib.ExitStack()

    sb_cnt = [0]

    def sb(es, name, shape, dt):
        sb_cnt[0] += 1
        return es.enter_context(nc.sbuf_tensor("s%d_%s" % (sb_cnt[0], name), list(shape), dt))

    banks = [es_all.enter_context(nc.psum_tensor("bk%d" % i, [128, 512], F32)) for i in range(8)]
    bank_rr = [0]
    bank_pool = [list(range(8))]

    def nbank():
        lst = bank_pool[0]
        i = lst[bank_rr[0] % len(lst)]
        bank_rr[0] += 1
        return banks[i], ("bk", i)

    def mm(out, lhsT, rhs, start, stop, reads, writes):
        return fw.op("pe", lambda: nc.tensor.matmul(out, lhsT=lhsT, rhs=rhs, start=start, stop=stop), reads, writes)

    def act(out, in_, func, reads, writes, eng="act", **kw):
        return fw.op("act", lambda: nc.scalar.activation(out=out, in_=in_, func=func, **kw), reads, writes)

    def tt(e, out, in0, in1, op, reads, writes):
        return fw.op(e, lambda: fw.eng[e].tensor_tensor(out=out, in0=in0, in1=in1, op=op), reads, writes)

    def ts(e, out, in0, s1, s2, op0, op1, reads, writes):
        if op1 is None:
            return fw.op(e, lambda: fw.eng[e].tensor_scalar(out=out, in0=in0, scalar1=s1, scalar2=None, op0=op0), reads, writes)
        return fw.op(e, lambda: fw.eng[e].tensor_scalar(out=out, in0=in0, scalar1=s1, scalar2=s2, op0=op0, op1=op1), reads, writes)

    def cp(e, out, in_, reads, writes):
        if e == "act":
            return act(out, in_, AF.Copy, reads, writes)
        return fw.op(e, lambda: fw.eng[e].tensor_copy(out=out, in_=in_), reads, writes)

    def memset(e, ap, val, writes):
        return fw.op(e, lambda: fw.eng[e].memset(ap, val), (), writes)

    idf = sb(es_all, "idf", [128, 128], F32)
    idb = sb(es_all, "idb", [128, 128], BF16)
    U = sb(es_all, "U", [128, 128], F32)
    Ls = sb(es_all, "Ls", [128, 128], F32)
    ones_f = sb(es_all, "ones_f", [128, 128], F32)
    negm = sb(es_all, "negm", [128, 128], BF16)
    ones_row = sb(es_all, "ones_row", [1, 128], BF16)
    gp_fm = sb(es_all, "gp_fm", [128, 8], F32)
    sh_fm = sb(es_all, "sh_fm", [128, 8], F32)
    gate_bc = sb(es_all, "gate_bc", [128, D], F32)
    Ftm = sb(es_all, "Ftm", [128, NB, NH], F32)
    refbc = sb(es_all, "refbc", [128, NT, NH], F32)
    attng_fm = sb(es_all, "attng_fm", [128, 8], F32)
    ssmg_fm = sb(es_all, "ssmg_fm", [128, 8], F32)
    wms = sb(es_all, "wms", [65, 64], BF16)

    memset("pool", idf[:], 0.0, ["idf"])
    fw.op("pool", lambda: nc.gpsimd.affine_select(out=idf[:], in_=idf[:], pattern=[[1, 128]], compare_op=ALU.not_equal,
                                                  fill=1.0, base=0, channel_multiplier=-1), ["idf"], ["idf"])
    cp("dve", idb[:], idf[:], ["idf"], ["idb"])
    memset("pool", ones_f[:], 1.0, ["ones_f"])
    fw.op("pool", lambda: nc.gpsimd.affine_select(out=U[:], in_=ones_f[:], pattern=[[1, 128]], compare_op=ALU.is_ge,
                                                  fill=0.0, base=0, channel_multiplier=-1), ["ones_f"], ["U"])
    fw.op("pool", lambda: nc.gpsimd.affine_select(out=Ls[:], in_=ones_f[:], pattern=[[-1, 128]], compare_op=ALU.is_gt,
                                                  fill=0.0, base=0, channel_multiplier=1), ["ones_f"], ["Ls"])
    ts("dve", negm[:], Ls[:], -30000.0, None, ALU.mult, None, ["Ls"], ["negm"])
    memset("dve", ones_row[:], 1.0, ["ones_row"])
    memset("dve", wms[0:64, :], 1.0 / 64.0, ["wms"])
    memset("dve", wms[64:65, :], EPS, ["wms"])
    fw.dma("sp", "c0", attng_fm[:], attng_fm_d[:, :], writes=["attng_fm"])
    fw.dma("sp", "c1", ssmg_fm[:], ssmg_fm_d[:, :], writes=["ssmg_fm"])

    with contextlib.ExitStack() as es:
        wada = sb(es, "wada", [128, 8, 3 * D], F32)
        c_fm = sb(es, "c_fm", [128, 8], F32)
        sc = sb(es, "sc", [128, 8], F32)
        scb = sb(es, "scb", [128, 8, 128], F32)
        ng_fm = sb(es, "ng_fm", [128, 8], F32)
        bada_fm = sb(es, "bada_fm", [128, 24], F32)
        bgate = sb(es, "bgate", [128, D], F32)
        modT = sb(es, "modT", [128, 16], F32)
        for kc in range(8):
            fw.dma("sp" if kc % 2 == 0 else "act", ("wada", kc), wada[:, kc, :], wada_d[kc * 128:(kc + 1) * 128, :],
                   writes=[("wada", kc)])
        fw.dma("sp", "p0a", c_fm[:], c_fm_d[:, :], writes=["c_fm"])
        fw.dma("sp", "p0b", ng_fm[:], ng_fm_d[:, :], writes=["ng_fm"])
        fw.dma("sp", "p0c", bada_fm[:], bada_fm_d[:, :], writes=["bada_fm"])
        fw.dma("sp", "p0d", bgate[:], bgate_bc_d[:, :], writes=["bgate"])
        act(sc[:], c_fm[:], AF.Silu, ["c_fm"], ["sc"])
        cp("dve", scb[:], sc[:].unsqueeze(2).to_broadcast([128, 8, 128]), ["sc"], ["scb"])
        bk, bkk = nbank()
        for n in range(16):
            for kc in range(8):
                mm(bk[:, n:n + 1], wada[:, kc, n * 128:(n + 1) * 128], sc[:, kc:kc + 1], kc == 0, kc == 7,
                   [("wada", kc), "sc"], [bkk])
        tt("dve", modT[:], bk[:, 0:16], bada_fm[:, 0:16], ALU.add, [bkk, "bada_fm"], ["modT"])
        cp("dve", sh_fm[:], modT[:, 0:8], ["modT"], ["sh_fm"])
        fw.op("dve", lambda: nc.vector.scalar_tensor_tensor(out=gp_fm[:], in0=modT[:, 8:16], scalar=1.0, in1=ng_fm[:],
                                                            op0=ALU.add, op1=ALU.mult), ["modT", "ng_fm"], ["gp_fm"])
        for half in range(2):
            bk, bkk = nbank()
            for kc in range(8):
                mm(bk[:, :], scb[:, kc, :], wada[:, kc, 2048 + half * 512:2048 + (half + 1) * 512], kc == 0, kc == 7,
                   [("wada", kc), "scb"], [bkk])
            tt("dve", gate_bc[:, half * 512:(half + 1) * 512], bk[:, :], bgate[:, half * 512:(half + 1) * 512], ALU.add,
               [bkk, "bgate"], ["gate_bc"])
        fw.barrier()

    def make_norm(es):
        st = {}
        st["xblk"] = [sb(es, "xblk%d" % i, [128, D], F32) for i in range(2)]
        st["junk"] = sb(es, "junk", [128, D], BF16)
        st["xn"] = sb(es, "xn", [128, D], BF16)
        st["stat"] = sb(es, "stat", [128, 4], F32)
        st["hT"] = sb(es, "hT", [128, 8, 128], BF16)
        return st

    def issue_x_load(st, b):
        slot = b % 2
        fw.dma("sp", ("xld", slot), st["xblk"][slot][:], x_d[b * 128:(b + 1) * 128, :], writes=[("xblk", slot)])

    def compute_hT(st, b):
        slot = b % 2
        xb = st["xblk"][slot]
        stat = st["stat"]
        act(st["junk"][:], xb[:], AF.Square, [("xblk", slot)], ["junk", "ssq"], accum_out=stat[:, 0:1])
        act(stat[:, 1:2], stat[:, 0:1], AF.Ln, ["ssq"], ["lnv"], scale=1.0 / D, bias=EPS)
        act(stat[:, 2:3], stat[:, 1:2], AF.Exp, ["lnv"], ["rstd"], scale=-0.5)
        ts("dve", st["xn"][:], xb[:], stat[:, 2:3], None, ALU.mult, None, [("xblk", slot), "rstd"], ["xn"])
        bk, bkk = nbank()
        bkb = bk[:].bitcast(BF16)
        for kc in range(8):
            fw.op("pe", lambda kc=kc: nc.tensor.transpose(out=bkb[:, kc * 128:(kc + 1) * 128],
                                                          in_=st["xn"][:, kc * 128:(kc + 1) * 128], identity=idb[:]),
                  ["xn", "idb"], [bkk])
        for kc in range(8):
            ts("dve", st["hT"][:, kc, :], bkb[:, kc * 128:(kc + 1) * 128], gp_fm[:, kc:kc + 1], sh_fm[:, kc:kc + 1],
               ALU.mult, ALU.add, [bkk, "gp_fm", "sh_fm"], ["hT"])

    def load_w_cols(es, wdst, col0, ncols, key):
        stg = [sb(es, "wstg%d" % i, [128, 2056], F32) for i in range(2)]
        n = 0
        for kc in range(8):
            c = 0
            while c < ncols:
                w = min(2056, ncols - c)
                s = n % 2
                fw.dma("sp" if s == 0 else "act", ("wstg", s), stg[s][:, 0:w],
                       win_d[kc * 128:(kc + 1) * 128, col0 + c:col0 + c + w], writes=[("wstg", s)])
                cp("pool" if s == 0 else "dve", wdst[:, kc, c:c + w], stg[s][:, 0:w], [("wstg", s)], [key])
                c += w
                n += 1

    with contextlib.ExitStack() as es:
        w1 = sb(es, "w1", [128, 8, 2576], BF16)
        dW = sb(es, "dW", [128, 4, 12, 128], BF16)
        cbrow = sb(es, "cbrow", [1, 1536], BF16)
        with contextlib.ExitStack() as es2:
            load_w_cols(es2, w1, 0, 2576, "w1")
            cwbc = sb(es2, "cwbc", [128, 4 * 1536], F32)
            cbrow_f = sb(es2, "cbrow_f", [1, 1536], F32)
            fw.dma("sp", "p1a", cwbc[:], convw_bc_d[:, :], writes=["cwbc"])
            fw.dma("sp", "p1b", cbrow_f[:], convb_row_d[:, :], writes=["cbrow_f"])
            for k in range(4):
                tt("dve", dW[:, k, :, :], idf[:].unsqueeze(1).to_broadcast([128, 12, 128]),
                   cwbc[:, k * 1536:(k + 1) * 1536].rearrange("p (c j) -> p c j", j=128), ALU.mult,
                   ["idf", "cwbc"], ["dW"])
            cp("dve", cbrow[:], cbrow_f[:], ["cbrow_f"], ["cbrow"])
            fw.barrier()
        st = make_norm(es)
        dtb = sb(es, "dtb", [128, NH], F32)
        a_bc = sb(es, "a_bc", [128, NH], F32)
        dskip = sb(es, "dskip", [128, NH], F32)
        fw.dma("sp", "p1c", dtb[:], dtb_bc_d[:, :], writes=["dtb"])
        fw.dma("sp", "p1d", a_bc[:], alog_bc_d[:, :], writes=["a_bc"])
        fw.dma("sp", "p1e", dskip[:], dskip_bc_d[:, :], writes=["dskip"])
        act(a_bc[:], a_bc[:], AF.Exp, ["a_bc"], ["a_bc"])
        ts("dve", a_bc[:], a_bc[:], -1.0, None, ALU.mult, None, ["a_bc"], ["a_bc"])
        uT = sb(es, "uT", [128, 12, 131], BF16)
        xcT = sb(es, "xcT", [128, 12, 128], BF16)
        sz = sb(es, "sz", [128, D], BF16)
        sm = sb(es, "sm", [128, 10, NH], F32)
        xc_tm = sb(es, "xc_tm", [128, D], BF16)
        B_tm = sb(es, "B_tm", [128, 256], BF16)
        xdt = sb(es, "xdt", [128, D], BF16)
        xdtd = sb(es, "xdtd", [128, D], BF16)
        xd = sb(es, "xd", [128, D], BF16)
        cbm = sb(es, "cbm", [128, 2, 128], BF16)
        rhs_all = sb(es, "rhs_all", [128, 8, 128], F32)
        expseg = sb(es, "expseg", [128, 8, 128], BF16)
        LT = sb(es, "LT", [128, NH, 128], BF16)
        yoff_s = sb(es, "yoff_s", [128, D], F32)
        ybuf = sb(es, "ybuf", [128, D], F32)
        y2 = sb(es, "y2", [128, D], F32)
        ybf = sb(es, "ybf", [128, D], BF16)
        state = sb(es, "state", [128, D], F32)
        state_bf = sb(es, "state_bf", [128, D], BF16)
        stmp = sb(es, "stmp", [128, D], F32)
        gst = sb(es, "gst", [128, 8], F32)
        yT = sb(es, "yT", [128, 8, 512], BF16)
        memset("dve", uT[:], 0.0, ["uT"])
        memset("dve", state[:], 0.0, ["state"])
        memset("pool", state_bf[:], 0.0, ["state_bf"])
        SM_DTR, SM_DT, SM_DA, SM_CUM, SM_D, SM_DTE, SM_DTD, SM_E, SM_ECUM, SM_EDEC = range(10)
        issue_x_load(st, 0)
        for b in range(NB):
            if b + 1 < NB:
                issue_x_load(st, b + 1)
            compute_hT(st, b)
            hT = st["hT"]
            for g4 in range(3):
                bk, bkk = nbank()
                for j in range(4):
                    cc = g4 * 4 + j
                    for kc in range(8):
                        mm(bk[:, j * 128:(j + 1) * 128], w1[:, kc, 1024 + cc * 128:1024 + (cc + 1) * 128], hT[:, kc, :],
                           kc == 0, kc == 7, ["w1", "hT"], [bkk])
                cp("act", uT[:, g4 * 4:(g4 + 1) * 4, 3:131], bk[:, :].rearrange("p (a t) -> p a t", t=128), [bkk], ["uT"])
            for g4 in range(3):
                bk, bkk = nbank()
                for j in range(4):
                    cc = g4 * 4 + j
                    for k in range(4):
                        mm(bk[:, j * 128:(j + 1) * 128], dW[:, k, cc, :], uT[:, cc, k:k + 128], k == 0, False,
                           ["dW", "uT"], [bkk])
                    mm(bk[:, j * 128:(j + 1) * 128], cbrow[0:1, cc * 128:(cc + 1) * 128], ones_row[0:1, :], False, True,
                       ["cbrow", "ones_row"], [bkk])
                act(xcT[:, g4 * 4:(g4 + 1) * 4, :], bk[:, :].rearrange("p (a t) -> p a t", t=128), AF.Silu, [bkk], ["xcT"])
            cp("dve", uT[:, :, 0:3], uT[:, :, 128:131], ["uT"], ["uT"])
            for g in range(2):
                bk, bkk = nbank()
                for kc in range(8):
                    mm(bk[:, :], hT[:, kc, :], w1[:, kc, g * 512:(g + 1) * 512], kc == 0, kc == 7, ["w1", "hT"], [bkk])
                act(sz[:, g * 512:(g + 1) * 512], bk[:, :], AF.Silu, [bkk], ["sz"])
            bk, bkk = nbank()
            for kc in range(8):
                mm(bk[:, 0:NH], hT[:, kc, :], w1[:, kc, 2560:2576], kc == 0, kc == 7, ["w1", "hT"], [bkk])
            tt("dve", sm[:, SM_DTR, :], bk[:, 0:NH], dtb[:], ALU.add, [bkk, "dtb"], ["dtr"])
            act(sm[:, SM_E, :], sm[:, SM_DTR, :], AF.Exp, ["dtr"], ["dte_e"])
            act(sm[:, SM_DT, :], sm[:, SM_E, :], AF.Ln, ["dte_e"], ["dt"], bias=1.0)
            tt("dve", sm[:, SM_DA, :], sm[:, SM_DT, :], a_bc[:], ALU.mult, ["dt", "a_bc"], ["da"])
            bk, bkk = nbank()
            mm(bk[:, 0:NH], U[:], sm[:, SM_DA, :], True, True, ["U", "da"], [bkk])
            mm(bk[:, NH:2 * NH], ones_f[:], sm[:, SM_DA, :], True, True, ["ones_f", "da"], [bkk])
            cp("dve", sm[:, SM_CUM, :], bk[:, 0:NH], [bkk], ["cum"])
            tt("dve", sm[:, SM_D, :], bk[:, NH:2 * NH], sm[:, SM_CUM, :], ALU.subtract, [bkk, "cum"], ["dd"])
            act(sm[:, SM_DTE, :], sm[:, SM_D, :], AF.Exp, ["dd"], ["dte"])
            act(sm[:, SM_EDEC, :], bk[:, NH:2 * NH], AF.Exp, [bkk], ["edec"])
            act(sm[:, SM_ECUM, :], sm[:, SM_CUM, :], AF.Exp, ["cum"], ["ecum"])
            tt("dve", sm[:, SM_DTD, :], sm[:, SM_DT, :], sm[:, SM_DTE, :], ALU.mult, ["dt", "dte"], ["dtd"])
            bk, bkk = nbank()
            bkb = bk[:].bitcast(BF16)
            for cc in range(8):
                fw.op("pe", lambda cc=cc: nc.tensor.transpose(out=bkb[:, cc * 128:(cc + 1) * 128], in_=xcT[:, cc, :],
                                                              identity=idb[:]), ["xcT", "idb"], [bkk])
            cp("act", xc_tm[:], bkb[:, :], [bkk], ["xc_tm"])
            bk, bkk = nbank()
            bkb = bk[:].bitcast(BF16)
            for g in range(2):
                fw.op("pe", lambda g=g: nc.tensor.transpose(out=bkb[:, g * 128:(g + 1) * 128], in_=xcT[:, 8 + g, :],
                                                            identity=idb[:]), ["xcT", "idb"], [bkk])
            cp("act", B_tm[:], bkb[:, 0:256], [bkk], ["B_tm"])
            xc3 = xc_tm[:].rearrange("p (h d) -> p h d", d=64)

            def bc64(ap16):
                return ap16.unsqueeze(2).to_broadcast([128, NH, 64])
            tt("dve", xdt[:].rearrange("p (h d) -> p h d", d=64), xc3, bc64(sm[:, SM_DT, :]), ALU.mult, ["xc_tm", "dt"], ["xdt"])
            tt("dve", xdtd[:].rearrange("p (h d) -> p h d", d=64), xc3, bc64(sm[:, SM_DTD, :]), ALU.mult, ["xc_tm", "dtd"], ["xdtd"])
            tt("pool", xd[:].rearrange("p (h d) -> p h d", d=64), xc3, bc64(dskip[:]), ALU.mult, ["xc_tm", "dskip"], ["xd"])
            bk, bkk = nbank()
            for g in range(2):
                mm(bk[:, g * 128:(g + 1) * 128], xcT[:, 8 + g, :], xcT[:, 10 + g, :], True, True, ["xcT"], [bkk])
            tt("dve", cbm[:], bk[:, 0:256].rearrange("p (g l) -> p g l", l=128), U[:].unsqueeze(1).to_broadcast([128, 2, 128]),
               ALU.mult, [bkk, "U"], ["cbm"])
            for g in range(2):
                tt("dve", rhs_all[:], U[:].unsqueeze(1).to_broadcast([128, 8, 128]),
                   sm[:, SM_DA, g * 8:(g + 1) * 8].unsqueeze(2).to_broadcast([128, 8, 128]), ALU.mult, ["U", "da"], ["rhs_all"])
                for q4 in range(2):
                    bk, bkk = nbank()
                    mm(bk[:, :], Ls[:], rhs_all[:, q4 * 4:(q4 + 1) * 4, :], True, True, ["Ls", "rhs_all"], [bkk])
                    act(expseg[:, q4 * 4:(q4 + 1) * 4, :], bk[:, :].rearrange("p (h l) -> p h l", l=128), AF.Exp, [bkk], ["expseg"])
                tt("pool", LT[:, g * 8:(g + 1) * 8, :], expseg[:], cbm[:, g, :].unsqueeze(1).to_broadcast([128, 8, 128]), ALU.mult,
                   ["expseg", "cbm"], [("LT", g)])
            for g in range(2):
                gs = slice(g * 512, (g + 1) * 512)
                bky, bkyk = nbank()
                mm(bky[:, :], idb[:], xd[:, gs], True, False, ["idb", "xd"], [bkyk])
                for r in range(8):
                    h = g * 8 + r
                    mm(bky[:, r * 64:(r + 1) * 64], LT[:, h, :], xdt[:, h * 64:(h + 1) * 64], False, r == 7,
                       [("LT", g), "xdt"], [bkyk])
                bko, bkok = nbank()
                mm(bko[:, :], xcT[:, 10 + g, :], state_bf[:, gs], True, True, ["xcT", ("state_bf", g)], [bkok])
                tt("dve", yoff_s[:, gs].rearrange("p (h d) -> p h d", d=64), bko[:, :].rearrange("p (h d) -> p h d", d=64),
                   sm[:, SM_ECUM, g * 8:(g + 1) * 8].unsqueeze(2).to_broadcast([128, 8, 64]), ALU.mult, [bkok, "ecum"], [("yoff", g)])
                tt("dve", ybuf[:, gs], bky[:, :], yoff_s[:, gs], ALU.add, [bkyk, ("yoff", g)], [("ybuf", g)])
                bks, bksk = nbank()
                mm(bks[:, :], B_tm[:, g * 128:(g + 1) * 128], xdtd[:, gs], True, True, ["B_tm", "xdtd"], [bksk])
                tt("pool", stmp[:, gs].rearrange("p (h d) -> p h d", d=64), state[:, gs].rearrange("p (h d) -> p h d", d=64),
                   sm[:, SM_EDEC, g * 8:(g + 1) * 8].unsqueeze(2).to_broadcast([128, 8, 64]), ALU.mult, [("state", g), "edec"], [("stmp", g)])
                tt("dve", state[:, gs], bks[:, :], stmp[:, gs], ALU.add, [bksk, ("stmp", g)], [("state", g)])
                cp("pool", state_bf[:, gs], state[:, gs], [("state", g)], [("state_bf", g)])
                tt("pool", y2[:, gs], ybuf[:, gs], sz[:, gs], ALU.mult, [("ybuf", g), "sz"], [("y2", g)])
                act(st["junk"][:, gs], y2[:, gs], AF.Square, [("y2", g)], ["junk", ("gssq", g)], accum_out=gst[:, g:g + 1])
                act(gst[:, 2 + g:3 + g], gst[:, g:g + 1], AF.Ln, [("gssq", g)], [("gln", g)], scale=1.0 / 512.0, bias=EPS)
                act(gst[:, 4 + g:5 + g], gst[:, 2 + g:3 + g], AF.Exp, [("gln", g)], [("grstd", g)], scale=-0.5)
                ts("dve", ybf[:, gs], y2[:, gs], gst[:, 4 + g:5 + g], None, ALU.mult, None, [("y2", g), ("grstd", g)], ["ybf"])
            bk, bkk = nbank()
            bkb = bk[:].bitcast(BF16)
            for cc in range(8):
                fw.op("pe", lambda cc=cc: nc.tensor.transpose(out=bkb[:, cc * 128:(cc + 1) * 128], in_=ybf[:, cc * 128:(cc + 1) * 128],
                                                              identity=idb[:]), ["ybf", "idb"], [bkk])
            q = b % 4
            cp("act", yT[:, :, q * 128:(q + 1) * 128], bkb[:, :].rearrange("p (c t) -> p c t", t=128), [bkk], ["yT"])
            if q == 3:
                t0 = (b - 3) * 128
                fw.dma("pool", "yTst", yT_s[:, t0:t0 + 512].rearrange("(c p) t -> p c t", p=128), yT[:], reads=["yT"], writes=[])
        fw.barrier()

    with contextlib.ExitStack() as es:
        w2 = sb(es, "w2", [128, 8, 4112], BF16)
        with contextlib.ExitStack() as es2:
            load_w_cols(es2, w2, 2576, 4112, "w2")
            fw.barrier()
        st = make_norm(es)
        gq2 = sb(es, "gq2", [128, 1], F32)
        gk2 = sb(es, "gk2", [128, 1], F32)
        fb = sb(es, "fb", [128, NH], F32)
        fw.dma("sp", "p2a", gq2[:], gq2_d[:, :], writes=["gq2"])
        fw.dma("sp", "p2b", gk2[:], gk2_d[:, :], writes=["gk2"])
        fw.dma("sp", "p2c", fb[:], fb_bc_d[:, :], writes=["fb"])
        sqb = sb(es, "sqb", [128, 512], F32)
        qst = sb(es, "qst", [128, 4, 8], F32)
        qn = [sb(es, "qn%d" % i, [128, D], BF16) for i in range(2)]
        QTc = sb(es, "QTc", [128, 8, 512], BF16)
        KTc = sb(es, "KTc", [128, 8, 512], BF16)
        zsT = sb(es, "zsT", [128, 8, 512], BF16)
        QAc = sb(es, "QAc", [NH, 512], BF16)
        v_tm = [sb(es, "v_tm%d" % i, [128, NH, 65], BF16) for i in range(2)]
        fs = sb(es, "fs", [128, 6, NH], F32)
        carry = sb(es, "carry", [128, NH], F32)
        fq8 = sb(es, "fq8", [128, NH], BF16)
        for i in range(2):
            memset("dve", v_tm[i][:], 1.0, [("v_tm", i)])
        memset("dve", carry[:], 0.0, ["carry"])
        issue_x_load(st, 0)
        for b in range(NB):
            if b + 1 < NB:
                issue_x_load(st, b + 1)
            compute_hT(st, b)
            hT = st["hT"]
            q4 = b % 4
            for qi, (coff, gain, dstT, dkey) in enumerate(((0, gq2, QTc, "QTc"), (1024, gk2, KTc, "KTc"))):
                for g in range(2):
                    bk, bkk = nbank()
                    for kc in range(8):
                        mm(bk[:, :], hT[:, kc, :], w2[:, kc, coff + g * 512:coff + (g + 1) * 512], kc == 0, kc == 7, ["w2", "hT"], [bkk])
                    act(sqb[:], bk[:, :], AF.Square, [bkk], ["sqb"])
                    fw.op("dve", lambda: nc.vector.tensor_reduce(out=qst[:, 0, :], in_=sqb[:].rearrange("p (h d) -> p h d", d=64),
                                                                 axis=AX.X, op=ALU.add), ["sqb"], ["qss"])
                    act(qst[:, 1, :], qst[:, 0, :], AF.Ln, ["qss"], ["qln"], scale=1.0 / 64.0, bias=EPS)
                    act(qst[:, 2, :], qst[:, 1, :], AF.Exp, ["qln"], ["qrs"], scale=-0.5)
                    tt("dve", qn[qi][:, g * 512:(g + 1) * 512].rearrange("p (h d) -> p h d", d=64),
                       bk[:, :].rearrange("p (h d) -> p h d", d=64), qst[:, 2, :].unsqueeze(2).to_broadcast([128, 8, 64]), ALU.mult,
                       [bkk, "qrs"], [("qn", qi)])
                bk, bkk = nbank()
                bkb = bk[:].bitcast(BF16)
                for cc in range(8):
                    fw.op("pe", lambda cc=cc, qi=qi: nc.tensor.transpose(out=bkb[:, cc * 128:(cc + 1) * 128],
                                                                         in_=qn[qi][:, cc * 128:(cc + 1) * 128], identity=idb[:]),
                          [("qn", qi), "idb"], [bkk])
                ts("dve", dstT[:, :, q4 * 128:(q4 + 1) * 128], bkb[:, :].rearrange("p (c t) -> p c t", t=128), gain[:, 0:1], None,
                   ALU.mult, None, [bkk], [dkey])
            vs = b % 2
            for g in range(2):
                bk, bkk = nbank()
                for kc in range(8):
                    mm(bk[:, :], hT[:, kc, :], w2[:, kc, 2048 + g * 512:2048 + (g + 1) * 512], kc == 0, kc == 7, ["w2", "hT"], [bkk])
                cp("act", v_tm[vs][:, g * 8:(g + 1) * 8, 0:64], bk[:, :].rearrange("p (h d) -> p h d", d=64), [bkk], [("v_tm", vs)])
            fw.dma("pool", ("vst", vs), V_s[b * 128:(b + 1) * 128, :, :], v_tm[vs][:], reads=[("v_tm", vs)], writes=[])
            for g4 in range(2):
                bk, bkk = nbank()
                for j in range(4):
                    cc = g4 * 4 + j
                    for kc in range(8):
                        mm(bk[:, j * 128:(j + 1) * 128], w2[:, kc, 3072 + cc * 128:3072 + (cc + 1) * 128], hT[:, kc, :],
                           kc == 0, kc == 7, ["w2", "hT"], [bkk])
                act(zsT[:, g4 * 4:(g4 + 1) * 4, q4 * 128:(q4 + 1) * 128], bk[:, :].rearrange("p (a t) -> p a t", t=128), AF.Silu,
                    [bkk], ["zsT"])
            bk, bkk = nbank()
            for kc in range(8):
                mm(bk[:, 0:NH], hT[:, kc, :], w2[:, kc, 4096:4112], kc == 0, kc == 7, ["w2", "hT"], [bkk])
            tt("dve", fs[:, 0, :], bk[:, 0:NH], fb[:], ALU.add, [bkk, "fb"], ["f0"])
            act(fs[:, 1, :], fs[:, 0, :], AF.Exp, ["f0"], ["f1"], scale=-1.0)
            act(fs[:, 2, :], fs[:, 1, :], AF.Ln, ["f1"], ["f2"], bias=1.0)
            bk, bkk = nbank()
            mm(bk[:, 0:NH], U[:], fs[:, 2, :], True, True, ["U", "f2"], [bkk])
            mm(bk[:, NH:2 * NH], ones_f[:], fs[:, 2, :], True, True, ["ones_f", "f2"], [bkk])
            if q4 == 0:
                cp("dve", refbc[:, b // 4, :], carry[:], ["carry"], ["refbc"])
            tt("dve", Ftm[:, b, :], carry[:], bk[:, 0:NH], ALU.subtract, ["carry", bkk], ["Ftm"])
            tt("dve", fs[:, 3, :], Ftm[:, b, :], refbc[:, b // 4, :], ALU.subtract, ["Ftm", "refbc"], ["f3"])
            ts("dve", fq8[:], fs[:, 3, :], 8.0, None, ALU.mult, None, ["f3"], ["fq8"])
            tt("dve", carry[:], carry[:], bk[:, NH:2 * NH], ALU.subtract, ["carry", bkk], ["carry"])
            bk, bkk = nbank()
            bkb = bk[:].bitcast(BF16)
            fw.op("pe", lambda: nc.tensor.transpose(out=bkb[0:NH, 0:128], in_=fq8[:, :], identity=idb[:]), ["fq8", "idb"], [bkk])
            cp("dve", QAc[:, q4 * 128:(q4 + 1) * 128], bkb[0:NH, 0:128], [bkk], ["QAc"])
            if q4 == 3:
                t0 = (b - 3) * 128
                fw.dma("sp", "qst", QT_s[:, t0:t0 + 512].rearrange("(c p) t -> p c t", p=128), QTc[:], reads=["QTc"], writes=[])
                fw.dma("act", "kst", KT_s[:, t0:t0 + 512].rearrange("(c p) t -> p c t", p=128), KTc[:], reads=["KTc"], writes=[])
                fw.dma("pool", "zst", zsT_s[:, t0:t0 + 512].rearrange("(c p) t -> p c t", p=128), zsT[:], reads=["zsT"], writes=[])
                fw.dma("sp", "qast", QA_s[:, t0:t0 + 512], QAc[:], reads=["QAc"], writes=[])
        fw.barrier()

    with contextlib.ExitStack() as es:
        KTh = [sb(es, "KTh%d" % i, [65, L], BF16) for i in range(2)]
        QTh = [sb(es, "QTh%d" % i, [65, L], BF16) for i in range(2)]
        Vh = [sb(es, "Vh%d" % i, [128, NB, 65], BF16) for i in range(2)]
        ZSh = [sb(es, "ZSh%d" % i, [64, L], BF16) for i in range(2)]
        NPT = 4
        PT = [sb(es, "PT%d" % i, [128, 512], BF16) for i in range(NPT)]
        btab = [sb(es, "btab%d" % i, [128, NB], F32) for i in range(2)]
        oaug = [sb(es, "oaug%d" % i, [65, 512], F32) for i in range(2)]
        sqo = [sb(es, "sqo%d" % i, [65, 512], BF16) for i in range(2)]
        r1 = sb(es, "r1", [64, 512], F32)
        r2 = sb(es, "r2", [64, 512], F32)
        t1 = sb(es, "t1", [64, 512], F32)
        oT = [sb(es, "oT%d" % i, [64, 512], BF16) for i in range(2)]
        for i in range(2):
            memset("dve", KTh[i][64:65, :], 1.0, [("KTh", i)])
        bank_pool[0] = [3, 4, 5, 6, 7]
        bank_rr[0] = 0

        def load_head(h):
            s = h % 2
            c, r0 = h // 2, (h % 2) * 64
            nq = max(1, L // 2048)
            qs = L // nq
            for i in range(nq):
                sl = slice(i * qs, (i + 1) * qs)
                fw.dma("sp", ("kld", s), KTh[s][0:64, sl], KT_s[c * 128 + r0:c * 128 + r0 + 64, sl], reads=["KT_s"], writes=[("KTh", s)])
                fw.dma("sp", ("qld", s), QTh[s][0:64, sl], QT_s[c * 128 + r0:c * 128 + r0 + 64, sl], reads=["QT_s"], writes=[("QTh", s)])
                fw.dma("sp", ("zld", s), ZSh[s][:, sl], zsT_s[h * 64:(h + 1) * 64, sl], reads=["zsT_s"], writes=[("ZSh", s)])
            fw.dma("sp", ("qald", s), QTh[s][64:65, :], QA_s[h:h + 1, :], reads=["QA_s"], writes=[("QTh", s)])
            nv = max(1, NB // 8)
            bs = NB // nv
            for i in range(nv):
                fw.dma("pool", ("vld", s), Vh[s][:, i * bs:(i + 1) * bs, :],
                       V_s[i * bs * 128:(i + 1) * bs * 128, h, :].rearrange("(b p) e -> p b e", p=128),
                       reads=["V_s"], writes=[("Vh", s)])

        load_head(0)
        pt_rr = 0
        cnt = 0
        for h in range(NH):
            s = h % 2
            if h + 1 < NH:
                load_head(h + 1)
            for j in range(NT):
                os_ = cnt % 2
                cnt += 1
                nblk = 4 * j + 4
                ts("dve", btab[os_][:, 0:nblk], Ftm[:, 0:nblk, h], -1.0, refbc[:, j, h:h + 1], ALU.mult, ALU.add,
                   ["Ftm", "refbc"], [("btab", os_)])
                bko, bkok = banks[os_], ("bk", os_)
                for i in range(nblk):
                    m = i - 4 * j
                    c0 = max(m, 0) * 128
                    bk, bkk = nbank()
                    mm(bk[:, c0:512], KTh[s][0:65, i * 128:(i + 1) * 128], QTh[s][0:65, j * 512 + c0:(j + 1) * 512], True, m < 0,
                       [("KTh", s), ("QTh", s)], [bkk])
                    if m >= 0:
                        mm(bk[:, c0:c0 + 128], idb[:], negm[:], False, True, ["idb", "negm"], [bkk])
                    p = pt_rr % NPT
                    pt_rr += 1
                    act(PT[p][:, c0:512], bk[:, c0:512], AF.Exp, [bkk, ("btab", os_)], [("PT", p)], scale=0.125, bias=btab[os_][:, i:i + 1])
                    mm(bko[0:65, c0:512], Vh[s][:, i, :], PT[p][:, c0:512], i == 0, i == nblk - 1, [("Vh", s), ("PT", p)], [bkok])
                cp("dve", oaug[os_][:], bko[0:65, :], [bkok], [("oaug", os_)])
                tt("pool", sqo[os_][:], oaug[os_][:], oaug[os_][:], ALU.mult, [("oaug", os_)], [("sqo", os_)])
                bkm, bkmk = banks[2], ("bk", 2)
                mm(bkm[0:64, :], wms[:], sqo[os_][:], True, True, ["wms", ("sqo", os_)], [bkmk])
                act(r1[:], bkm[0:64, :], AF.Ln, [bkmk], ["r1"])
                act(r2[:], r1[:], AF.Exp, ["r1"], ["r2"], scale=-0.5)
                tt("dve", t1[:], oaug[os_][0:64, :], r2[:], ALU.mult, [("oaug", os_), "r2"], ["t1"])
                tt("pool", oT[os_][:], t1[:], ZSh[s][:, j * 512:(j + 1) * 512], ALU.mult, ["t1", ("ZSh", s)], [("oT", os_)])
                fw.dma("act", ("ost", os_), oT_s[h * 64:(h + 1) * 64, j * 512:(j + 1) * 512], oT[os_][:], reads=[("oT", os_)], writes=[])
        bank_pool[0] = list(range(8))
        fw.barrier()

    with contextlib.ExitStack() as es:
        wo = sb(es, "wo", [128, 16, D], BF16)
        with contextlib.ExitStack() as es2:
            stg = [sb(es2, "wostg%d" % i, [128, D], F32) for i in range(2)]
            for kc in range(16):
                s = kc % 2
                fw.dma("sp" if s == 0 else "act", ("wostg", s), stg[s][:], wout_d[kc * 128:(kc + 1) * 128, :], writes=[("wostg", s)])
                gsrc = ssmg_fm if kc < 8 else attng_fm
                ts("dve", wo[:, kc, :], stg[s][:], gsrc[:, kc % 8:kc % 8 + 1], None, ALU.mult, None, [("wostg", s)], ["wo"])
            fw.barrier()
        yo = [sb(es, "yo%d" % i, [128, 16, 512], BF16) for i in range(2)]
        xr = [sb(es, "xr%d" % i, [128, D], F32) for i in range(2)]
        tmpo = [sb(es, "tmpo%d" % i, [128, D], F32) for i in range(2)]
        ob = [sb(es, "ob%d" % i, [128, D], F32) for i in range(2)]

        def load_tile(j):
            s = j % 2
            fw.dma("sp", ("yold", s), yo[s][:, 0:8, :], yT_s[:, j * 512:(j + 1) * 512].rearrange("(c p) t -> p c t", p=128),
                   reads=["yT_s"], writes=[("yo", s)])
            fw.dma("act", ("yold2", s), yo[s][:, 8:16, :], oT_s[:, j * 512:(j + 1) * 512].rearrange("(c p) t -> p c t", p=128),
                   reads=["oT_s"], writes=[("yo", s)])

        load_tile(0)
        for j in range(NT):
            s = j % 2
            if j + 1 < NT:
                load_tile(j + 1)
            for q in range(4):
                b = j * 4 + q
                xs = b % 2
                fw.dma("pool", ("xrld", xs), xr[xs][:], x_d[b * 128:(b + 1) * 128, :], writes=[("xr", xs)])
                for half in range(2):
                    hs = slice(half * 512, (half + 1) * 512)
                    bk, bkk = nbank()
                    for kc in range(16):
                        mm(bk[:, :], yo[s][:, kc, q * 128:(q + 1) * 128], wo[:, kc, hs], kc == 0, kc == 15, [("yo", s), "wo"], [bkk])
                    tt("dve", tmpo[xs][:, hs], bk[:, :], gate_bc[:, hs], ALU.mult, [bkk, "gate_bc"], [("tmpo", xs, half)])
                    tt("pool", ob[xs][:, hs], tmpo[xs][:, hs], xr[xs][:, hs], ALU.add, [("tmpo", xs, half), ("xr", xs)], [("ob", xs)])
                fw.dma("sp", ("ost3", xs), out_d[b * 128:(b + 1) * 128, :], ob[xs][:], reads=[("ob", xs)], writes=[])
        fw.barrier()
    es_all.close()
    return nc, fw


def _layout_inputs(inp, L):
    f = lambda a: np.ascontiguousarray(np.asarray(a, dtype=np.float32))
    fm8 = lambda v: f(np.asarray(v).reshape(-1, 128).T)
    bc = lambda v: f(np.broadcast_to(np.asarray(v).reshape(1, -1), (128, np.asarray(v).size)))
    shared = {
        "ng_fm": fm8(inp["norm_gain"][0]),
        "w_ada": f(inp["w_ada"][0]),
        "bada_fm": fm8(inp["b_ada"][0]),
        "bgate_bc": bc(inp["b_ada"][0][2 * D:3 * D]),
        "w_in": f(inp["w_in"][0]),
        "convw_bc": bc(np.asarray(inp["conv_w"][0]).reshape(-1)),
        "convb_row": f(np.asarray(inp["conv_b"][0]).reshape(1, -1)),
        "dtb_bc": bc(inp["dt_bias"][0]),
        "alog_bc": bc(inp["a_log"][0]),
        "dskip_bc": bc(inp["d_skip"][0]),
        "fb_bc": bc(inp["forget_bias"][0]),
        "ssmg_fm": fm8(inp["ssm_norm_gain"][0]),
        "attng_fm": fm8(inp["attn_norm_gain"][0]),
        "gq2": f(np.tile(np.asarray(inp["q_norm_gain"][0]), 2).reshape(128, 1)),
        "gk2": f(np.tile(np.asarray(inp["k_norm_gain"][0]), 2).reshape(128, 1)),
        "w_out": f(inp["w_out"][0]),
    }
    x = np.asarray(inp["x"], dtype=np.float32)
    c = np.asarray(inp["c"], dtype=np.float32)
    maps = []
    for b in range(x.shape[0]):
        m = dict(shared)
        m["x"] = np.ascontiguousarray(x[b, :L])
        m["c_fm"] = fm8(c[b])
        maps.append(m)
    return maps


def kernel(**inputs):
    x = np.asarray(inputs["x"])
    B, L, _ = x.shape
    nc, _ = build_nc(L)
    in_maps = _layout_inputs(inputs, L)
    res = run_bass_kernel_spmd(nc, in_maps, core_ids=list(range(B)))
    out = np.stack([np.asarray(r["out"], dtype=np.float32) for r in res.results], axis=0)
    return out
```

```python
import numpy as np
import concourse.bass as bass
import concourse.mybir as mybir
from concourse.bass_utils import run_bass_kernel_spmd

F32 = mybir.dt.float32
BF16 = mybir.dt.bfloat16
AF = mybir.ActivationFunctionType
ALU = mybir.AluOpType
AX = mybir.AxisListType

D = 1024
NH = 16
EPS = 1e-6
SEQ = 8192
NCORES = 8


class FW:
    def __init__(self, nc):
        self.nc = nc
        self.eng = {"pe": nc.tensor, "act": nc.scalar, "dve": nc.vector, "pool": nc.gpsimd, "sp": nc.sync}
        self.esem = {k: nc.semaphore("sem_" + k).__enter__() for k in self.eng}
        self.tick = {k: 0 for k in self.eng}
        self.waited = {}
        self.last_w = {}
        self.readers = {}
        self.dma_sems = {}
        self.n_inst = 0
        self.n_wait = 0

    def _need(self, e, sig, pend):
        sem, val, src = sig
        if src == e and e in ("pe", "sp"):
            return
        k = (e, id(sem))
        if self.waited.get(k, 0) >= val:
            return
        self.waited[k] = val
        pend[id(sem)] = (sem, val)

    def _wait(self, e, sig):
        pend = {}
        self._need(e, sig, pend)
        for sem, val in pend.values():
            self.eng[e].wait_ge(sem, val)
            self.n_wait += 1

    def _deps(self, e, reads, writes):
        pend = {}
        for k in reads:
            s = self.last_w.get(k)
            if s is not None:
                self._need(e, s, pend)
        for k in writes:
            s = self.last_w.get(k)
            if s is not None:
                self._need(e, s, pend)
            for r in self.readers.get(k, ()):
                self._need(e, r, pend)
        return list(pend.values())

    def _commit(self, sig, reads, writes):
        for k in writes:
            self.last_w[k] = sig
            self.readers[k] = []
        for k in reads:
            if k in writes:
                continue
            lst = self.readers.setdefault(k, [])
            lst[:] = [r for r in lst if r[0] is not sig[0]]
            lst.append(sig)

    def op(self, e, fn, reads=(), writes=(), inline=True):
        pend = self._deps(e, reads, writes)
        inline = inline and e != "pe" and len(pend) > 0
        for sem, val in (pend[:-1] if inline else pend):
            self.eng[e].wait_ge(sem, val)
            self.n_wait += 1
        ins = fn()
        if inline:
            ins._wait_ge(pend[-1][0], pend[-1][1])
        self.tick[e] += 1
        ins.then_inc(self.esem[e], 1)
        self._commit((self.esem[e], self.tick[e], e), reads, writes)
        self.n_inst += 1
        return ins

    def dma(self, e, semkey, out, in_, reads=(), writes=()):
        pend = self._deps(e, reads, writes)
        for sem, val in pend:
            self.eng[e].wait_ge(sem, val)
            self.n_wait += 1
        if semkey not in self.dma_sems:
            self.dma_sems[semkey] = [self.nc.semaphore("dsem%d" % len(self.dma_sems)).__enter__(), 0]
        ent = self.dma_sems[semkey]
        ins = self.eng[e].dma_start(out=out, in_=in_)
        ent[1] += 16
        ins.then_inc(ent[0], 16)
        self._commit((ent[0], ent[1], "dma"), reads, writes)
        self.n_inst += 1
        return ins

    def barrier(self):
        for e in self.eng:
            for k, ent in self.dma_sems.items():
                if ent[1] > 0:
                    self._wait(e, (ent[0], ent[1], "dma"))
            for k in self.eng:
                if self.tick[k] > 0 and k != e:
                    self._wait(e, (self.esem[k], self.tick[k], k))
        self.last_w.clear()
        self.readers.clear()


def build_nc(L, debug=False):
    NB = L // 128
    NT = L // 512
    nc = bass.Bass("TRN2", target_bir_lowering=False)
    fw = FW(nc)

    def din(name, shape, dt=F32):
        return nc.dram_tensor(name, list(shape), dt, kind="ExternalInput").ap()

    def dscr(name, shape, dt):
        return nc.dram_tensor(name, list(shape), dt, kind=("ExternalOutput" if debug else "Internal")).ap()

    x_d = din("x", [L, D])
    c_fm_d = din("c_fm", [128, 8])
    ng_fm_d = din("ng_fm", [128, 8])
    wada_d = din("w_ada", [D, 3 * D])
    bada_fm_d = din("bada_fm", [128, 24])
    bgate_bc_d = din("bgate_bc", [128, D])
    win_d = din("w_in", [D, 6688])
    convw_bc_d = din("convw_bc", [128, 4 * 1536])
    convb_row_d = din("convb_row", [1, 1536])
    dtb_bc_d = din("dtb_bc", [128, NH])
    alog_bc_d = din("alog_bc", [128, NH])
    dskip_bc_d = din("dskip_bc", [128, NH])
    fb_bc_d = din("fb_bc", [128, NH])
    ssmg_fm_d = din("ssmg_fm", [128, 8])
    attng_fm_d = din("attng_fm", [128, 8])
    gq2_d = din("gq2", [128, 1])
    gk2_d = din("gk2", [128, 1])
    wout_d = din("w_out", [2 * D, D])
    out_d = nc.dram_tensor("out", [L, D], F32, kind="ExternalOutput").ap()

    yT_s = dscr("yT_s", [D, L], BF16)
    oT_s = dscr("oT_s", [D, L], BF16)
    zsT_s = dscr("zsT_s", [D, L], BF16)
    QT_s = dscr("QT_s", [D, L], BF16)
    KT_s = dscr("KT_s", [D, L], BF16)
    QA_s = dscr("QA_s", [NH, L], BF16)
    V_s = dscr("V_s", [L, NH, 65], BF16)

    import contextlib
    es_all = contextlib.ExitStack()

    sb_cnt = [0]

    def sb(es, name, shape, dt):
        sb_cnt[0] += 1
        return es.enter_context(nc.sbuf_tensor("s%d_%s" % (sb_cnt[0], name), list(shape), dt))

    banks = [es_all.enter_context(nc.psum_tensor("bk%d" % i, [128, 512], F32)) for i in range(8)]
    bank_rr = [0]
    bank_pool = [list(range(8))]

    def nbank():
        lst = bank_pool[0]
        i = lst[bank_rr[0] % len(lst)]
        bank_rr[0] += 1
        return banks[i], ("bk", i)

    def mm(out, lhsT, rhs, start, stop, reads, writes):
        return fw.op("pe", lambda: nc.tensor.matmul(out, lhsT=lhsT, rhs=rhs, start=start, stop=stop), reads, writes)

    def act(out, in_, func, reads, writes, eng="act", **kw):
        return fw.op("act", lambda: nc.scalar.activation(out=out, in_=in_, func=func, **kw), reads, writes,
                     inline=("accum_out" not in kw))

    def tt(e, out, in0, in1, op, reads, writes):
        return fw.op(e, lambda: fw.eng[e].tensor_tensor(out=out, in0=in0, in1=in1, op=op), reads, writes)

    def ts(e, out, in0, s1, s2, op0, op1, reads, writes):
        if op1 is None:
            return fw.op(e, lambda: fw.eng[e].tensor_scalar(out=out, in0=in0, scalar1=s1, scalar2=None, op0=op0), reads, writes)
        return fw.op(e, lambda: fw.eng[e].tensor_scalar(out=out, in0=in0, scalar1=s1, scalar2=s2, op0=op0, op1=op1), reads, writes)

    def cp(e, out, in_, reads, writes):
        if e == "act":
            return act(out, in_, AF.Copy, reads, writes)
        return fw.op(e, lambda: fw.eng[e].tensor_copy(out=out, in_=in_), reads, writes)

    def memset(e, ap, val, writes):
        return fw.op(e, lambda: fw.eng[e].memset(ap, val), (), writes)

    idf = sb(es_all, "idf", [128, 128], F32)
    idb = sb(es_all, "idb", [128, 128], BF16)
    U = sb(es_all, "U", [128, 128], F32)
    Ls = sb(es_all, "Ls", [128, 128], F32)
    ones_f = sb(es_all, "ones_f", [128, 128], F32)
    negm = sb(es_all, "negm", [128, 128], BF16)
    ones_row = sb(es_all, "ones_row", [1, 128], BF16)
    gp_fm = sb(es_all, "gp_fm", [128, 8], F32)
    sh_fm = sb(es_all, "sh_fm", [128, 8], F32)
    gate_bc = sb(es_all, "gate_bc", [128, D], F32)
    Ftm = sb(es_all, "Ftm", [128, NB, NH], F32)
    refbc = sb(es_all, "refbc", [128, NT, NH], F32)
    attng_fm = sb(es_all, "attng_fm", [128, 8], F32)
    ssmg_fm = sb(es_all, "ssmg_fm", [128, 8], F32)
    wms = sb(es_all, "wms", [65, 64], BF16)

    memset("pool", idf[:], 0.0, ["idf"])
    fw.op("pool", lambda: nc.gpsimd.affine_select(out=idf[:], in_=idf[:], pattern=[[1, 128]], compare_op=ALU.not_equal,
                                                  fill=1.0, base=0, channel_multiplier=-1), ["idf"], ["idf"])
    cp("dve", idb[:], idf[:], ["idf"], ["idb"])
    memset("pool", ones_f[:], 1.0, ["ones_f"])
    fw.op("pool", lambda: nc.gpsimd.affine_select(out=U[:], in_=ones_f[:], pattern=[[1, 128]], compare_op=ALU.is_ge,
                                                  fill=0.0, base=0, channel_multiplier=-1), ["ones_f"], ["U"])
    fw.op("pool", lambda: nc.gpsimd.affine_select(out=Ls[:], in_=ones_f[:], pattern=[[-1, 128]], compare_op=ALU.is_gt,
                                                  fill=0.0, base=0, channel_multiplier=1), ["ones_f"], ["Ls"])
    ts("dve", negm[:], Ls[:], -30000.0, None, ALU.mult, None, ["Ls"], ["negm"])
    memset("dve", ones_row[:], 1.0, ["ones_row"])
    memset("dve", wms[0:64, :], 1.0 / 64.0, ["wms"])
    memset("dve", wms[64:65, :], EPS, ["wms"])
    fw.dma("sp", "c0", attng_fm[:], attng_fm_d[:, :], writes=["attng_fm"])
    fw.dma("sp", "c1", ssmg_fm[:], ssmg_fm_d[:, :], writes=["ssmg_fm"])

    with contextlib.ExitStack() as es:
        wada = sb(es, "wada", [128, 8, 3 * D], F32)
        c_fm = sb(es, "c_fm", [128, 8], F32)
        sc = sb(es, "sc", [128, 8], F32)
        scb = sb(es, "scb", [128, 8, 128], F32)
        ng_fm = sb(es, "ng_fm", [128, 8], F32)
        bada_fm = sb(es, "bada_fm", [128, 24], F32)
        bgate = sb(es, "bgate", [128, D], F32)
        modT = sb(es, "modT", [128, 16], F32)
        for kc in range(8):
            fw.dma("sp" if kc % 2 == 0 else "act", ("wada", kc), wada[:, kc, :], wada_d[kc * 128:(kc + 1) * 128, :],
                   writes=[("wada", kc)])
        fw.dma("sp", "p0a", c_fm[:], c_fm_d[:, :], writes=["c_fm"])
        fw.dma("sp", "p0b", ng_fm[:], ng_fm_d[:, :], writes=["ng_fm"])
        fw.dma("sp", "p0c", bada_fm[:], bada_fm_d[:, :], writes=["bada_fm"])
        fw.dma("sp", "p0d", bgate[:], bgate_bc_d[:, :], writes=["bgate"])
        act(sc[:], c_fm[:], AF.Silu, ["c_fm"], ["sc"])
        cp("dve", scb[:], sc[:].unsqueeze(2).to_broadcast([128, 8, 128]), ["sc"], ["scb"])
        bk, bkk = nbank()
        for n in range(16):
            for kc in range(8):
                mm(bk[:, n:n + 1], wada[:, kc, n * 128:(n + 1) * 128], sc[:, kc:kc + 1], kc == 0, kc == 7,
                   [("wada", kc), "sc"], [bkk])
        tt("dve", modT[:], bk[:, 0:16], bada_fm[:, 0:16], ALU.add, [bkk, "bada_fm"], ["modT"])
        cp("dve", sh_fm[:], modT[:, 0:8], ["modT"], ["sh_fm"])
        fw.op("dve", lambda: nc.vector.scalar_tensor_tensor(out=gp_fm[:], in0=modT[:, 8:16], scalar=1.0, in1=ng_fm[:],
                                                            op0=ALU.add, op1=ALU.mult), ["modT", "ng_fm"], ["gp_fm"])
        for half in range(2):
            bk, bkk = nbank()
            for kc in range(8):
                mm(bk[:, :], scb[:, kc, :], wada[:, kc, 2048 + half * 512:2048 + (half + 1) * 512], kc == 0, kc == 7,
                   [("wada", kc), "scb"], [bkk])
            tt("dve", gate_bc[:, half * 512:(half + 1) * 512], bk[:, :], bgate[:, half * 512:(half + 1) * 512], ALU.add,
               [bkk, "bgate"], ["gate_bc"])
        fw.barrier()

    def make_norm(es):
        st = {}
        st["xblk"] = [sb(es, "xblk%d" % i, [128, D], F32) for i in range(2)]
        st["junk"] = sb(es, "junk", [128, D], BF16)
        st["xn"] = sb(es, "xn", [128, D], BF16)
        st["stat"] = sb(es, "stat", [128, 4], F32)
        st["hT"] = [sb(es, "hT%d" % i, [128, 8, 128], BF16) for i in range(2)]
        return st

    def issue_x_load(st, b):
        slot = b % 2
        fw.dma("sp", ("xld", slot), st["xblk"][slot][:], x_d[b * 128:(b + 1) * 128, :], writes=[("xblk", slot)])

    def compute_hT(st, b):
        slot = b % 2
        xb = st["xblk"][slot]
        stat = st["stat"]
        act(st["junk"][:], xb[:], AF.Square, [("xblk", slot)], ["junk", "ssq"], accum_out=stat[:, 0:1])
        act(stat[:, 1:2], stat[:, 0:1], AF.Ln, ["ssq"], ["lnv"], scale=1.0 / D, bias=EPS)
        act(stat[:, 2:3], stat[:, 1:2], AF.Exp, ["lnv"], ["rstd"], scale=-0.5)
        ts("dve", st["xn"][:], xb[:], stat[:, 2:3], None, ALU.mult, None, [("xblk", slot), "rstd"], ["xn"])
        bk, bkk = nbank()
        bkb = bk[:].bitcast(BF16)
        for kc in range(8):
            fw.op("pe", lambda kc=kc: nc.tensor.transpose(out=bkb[:, kc * 128:(kc + 1) * 128],
                                                          in_=st["xn"][:, kc * 128:(kc + 1) * 128], identity=idb[:]),
                  ["xn", "idb"], [bkk])
        for kc in range(8):
            ts("dve", st["hT"][slot][:, kc, :], bkb[:, kc * 128:(kc + 1) * 128], gp_fm[:, kc:kc + 1], sh_fm[:, kc:kc + 1],
               ALU.mult, ALU.add, [bkk, "gp_fm", "sh_fm"], [("hT", slot)])

    def load_w_cols(es, wdst, col0, ncols, key):
        stg = [sb(es, "wstg%d" % i, [128, 2056], F32) for i in range(2)]
        n = 0
        for kc in range(8):
            c = 0
            while c < ncols:
                w = min(2056, ncols - c)
                s = n % 2
                fw.dma("sp" if s == 0 else "act", ("wstg", s), stg[s][:, 0:w],
                       win_d[kc * 128:(kc + 1) * 128, col0 + c:col0 + c + w], writes=[("wstg", s)])
                cp("pool" if s == 0 else "dve", wdst[:, kc, c:c + w], stg[s][:, 0:w], [("wstg", s)], [key])
                c += w
                n += 1

    with contextlib.ExitStack() as es:
        w1 = sb(es, "w1", [128, 8, 2576], BF16)
        dW = sb(es, "dW", [128, 4, 12, 128], BF16)
        cbrow = sb(es, "cbrow", [1, 1536], BF16)
        with contextlib.ExitStack() as es2:
            load_w_cols(es2, w1, 0, 2576, "w1")
            cwbc = sb(es2, "cwbc", [128, 4 * 1536], F32)
            cbrow_f = sb(es2, "cbrow_f", [1, 1536], F32)
            fw.dma("sp", "p1a", cwbc[:], convw_bc_d[:, :], writes=["cwbc"])
            fw.dma("sp", "p1b", cbrow_f[:], convb_row_d[:, :], writes=["cbrow_f"])
            for k in range(4):
                tt("dve", dW[:, k, :, :], idf[:].unsqueeze(1).to_broadcast([128, 12, 128]),
                   cwbc[:, k * 1536:(k + 1) * 1536].rearrange("p (c j) -> p c j", j=128), ALU.mult,
                   ["idf", "cwbc"], ["dW"])
            cp("dve", cbrow[:], cbrow_f[:], ["cbrow_f"], ["cbrow"])
            fw.barrier()
        st = make_norm(es)
        dtb = sb(es, "dtb", [128, NH], F32)
        a_bc = sb(es, "a_bc", [128, NH], F32)
        dskip = sb(es, "dskip", [128, NH], F32)
        fw.dma("sp", "p1c", dtb[:], dtb_bc_d[:, :], writes=["dtb"])
        fw.dma("sp", "p1d", a_bc[:], alog_bc_d[:, :], writes=["a_bc"])
        fw.dma("sp", "p1e", dskip[:], dskip_bc_d[:, :], writes=["dskip"])
        act(a_bc[:], a_bc[:], AF.Exp, ["a_bc"], ["a_bc"])
        ts("dve", a_bc[:], a_bc[:], -1.0, None, ALU.mult, None, ["a_bc"], ["a_bc"])
        uT = sb(es, "uT", [128, 12, 131], BF16)
        xcT = sb(es, "xcT", [128, 12, 128], BF16)
        sz = sb(es, "sz", [128, D], BF16)
        sm = sb(es, "sm", [128, 10, NH], F32)
        xc_tm = sb(es, "xc_tm", [128, D], BF16)
        B_tm = sb(es, "B_tm", [128, 256], BF16)
        xdt = sb(es, "xdt", [128, D], BF16)
        xdtd = sb(es, "xdtd", [128, D], BF16)
        xd = sb(es, "xd", [128, D], BF16)
        cbm = sb(es, "cbm", [128, 2, 128], BF16)
        rhs_all = sb(es, "rhs_all", [128, 8, 128], F32)
        expseg = sb(es, "expseg", [128, 8, 128], BF16)
        LT = sb(es, "LT", [128, NH, 128], BF16)
        yoff_s = sb(es, "yoff_s", [128, D], F32)
        ybuf = sb(es, "ybuf", [128, D], F32)
        y2 = sb(es, "y2", [128, D], F32)
        ybf = sb(es, "ybf", [128, D], BF16)
        state = sb(es, "state", [128, D], F32)
        state_bf = sb(es, "state_bf", [128, D], BF16)
        stmp = sb(es, "stmp", [128, D], F32)
        gst = sb(es, "gst", [128, 8], F32)
        yT = sb(es, "yT", [128, 8, 512], BF16)
        memset("dve", uT[:], 0.0, ["uT"])
        memset("dve", state[:], 0.0, ["state"])
        memset("pool", state_bf[:], 0.0, ["state_bf"])
        SM_DTR, SM_DT, SM_DA, SM_CUM, SM_D, SM_DTE, SM_DTD, SM_E, SM_ECUM, SM_EDEC = range(10)
        issue_x_load(st, 0)
        if NB > 1:
            issue_x_load(st, 1)
        compute_hT(st, 0)
        for b in range(NB):
            hT = st["hT"][b % 2]
            hTk = ("hT", b % 2)
            for g4 in range(3):
                bk, bkk = nbank()
                for j in range(4):
                    cc = g4 * 4 + j
                    for kc in range(8):
                        mm(bk[:, j * 128:(j + 1) * 128], w1[:, kc, 1024 + cc * 128:1024 + (cc + 1) * 128], hT[:, kc, :],
                           kc == 0, kc == 7, ["w1", hTk], [bkk])
                cp("act", uT[:, g4 * 4:(g4 + 1) * 4, 3:131], bk[:, :].rearrange("p (a t) -> p a t", t=128), [bkk], ["uT"])
            for g in range(2):
                bk, bkk = nbank()
                for kc in range(8):
                    mm(bk[:, :], hT[:, kc, :], w1[:, kc, g * 512:(g + 1) * 512], kc == 0, kc == 7, ["w1", hTk], [bkk])
                act(sz[:, g * 512:(g + 1) * 512], bk[:, :], AF.Silu, [bkk], ["sz"])
            bk, bkk = nbank()
            for kc in range(8):
                mm(bk[:, 0:NH], hT[:, kc, :], w1[:, kc, 2560:2576], kc == 0, kc == 7, ["w1", hTk], [bkk])
            tt("dve", sm[:, SM_DTR, :], bk[:, 0:NH], dtb[:], ALU.add, [bkk, "dtb"], ["dtr"])
            act(sm[:, SM_E, :], sm[:, SM_DTR, :], AF.Exp, ["dtr"], ["dte_e"])
            act(sm[:, SM_DT, :], sm[:, SM_E, :], AF.Ln, ["dte_e"], ["dt"], bias=1.0)
            tt("dve", sm[:, SM_DA, :], sm[:, SM_DT, :], a_bc[:], ALU.mult, ["dt", "a_bc"], ["da"])
            for g4 in range(3):
                bk, bkk = nbank()
                for j in range(4):
                    cc = g4 * 4 + j
                    for k in range(4):
                        mm(bk[:, j * 128:(j + 1) * 128], dW[:, k, cc, :], uT[:, cc, k:k + 128], k == 0, False,
                           ["dW", "uT"], [bkk])
                    mm(bk[:, j * 128:(j + 1) * 128], cbrow[0:1, cc * 128:(cc + 1) * 128], ones_row[0:1, :], False, True,
                       ["cbrow", "ones_row"], [bkk])
                act(xcT[:, g4 * 4:(g4 + 1) * 4, :], bk[:, :].rearrange("p (a t) -> p a t", t=128), AF.Silu, [bkk], ["xcT"])
            cp("dve", uT[:, :, 0:3], uT[:, :, 128:131], ["uT"], ["uT"])
            bk, bkk = nbank()
            mm(bk[:, 0:NH], U[:], sm[:, SM_DA, :], True, True, ["U", "da"], [bkk])
            mm(bk[:, NH:2 * NH], ones_f[:], sm[:, SM_DA, :], True, True, ["ones_f", "da"], [bkk])
            cp("dve", sm[:, SM_CUM, :], bk[:, 0:NH], [bkk], ["cum"])
            tt("dve", sm[:, SM_D, :], bk[:, NH:2 * NH], sm[:, SM_CUM, :], ALU.subtract, [bkk, "cum"], ["dd"])
            act(sm[:, SM_DTE, :], sm[:, SM_D, :], AF.Exp, ["dd"], ["dte"])
            act(sm[:, SM_EDEC, :], bk[:, NH:2 * NH], AF.Exp, [bkk], ["edec"])
            act(sm[:, SM_ECUM, :], sm[:, SM_CUM, :], AF.Exp, ["cum"], ["ecum"])
            tt("dve", sm[:, SM_DTD, :], sm[:, SM_DT, :], sm[:, SM_DTE, :], ALU.mult, ["dt", "dte"], ["dtd"])
            if b + 1 < NB:
                compute_hT(st, b + 1)
            if b + 2 < NB:
                issue_x_load(st, b + 2)
            bk, bkk = nbank()
            bkb = bk[:].bitcast(BF16)
            for cc in range(8):
                fw.op("pe", lambda cc=cc: nc.tensor.transpose(out=bkb[:, cc * 128:(cc + 1) * 128], in_=xcT[:, cc, :],
                                                              identity=idb[:]), ["xcT", "idb"], [bkk])
            cp("act", xc_tm[:], bkb[:, :], [bkk], ["xc_tm"])
            bk, bkk = nbank()
            bkb = bk[:].bitcast(BF16)
            for g in range(2):
                fw.op("pe", lambda g=g: nc.tensor.transpose(out=bkb[:, g * 128:(g + 1) * 128], in_=xcT[:, 8 + g, :],
                                                            identity=idb[:]), ["xcT", "idb"], [bkk])
            cp("act", B_tm[:], bkb[:, 0:256], [bkk], ["B_tm"])
            xc3 = xc_tm[:].rearrange("p (h d) -> p h d", d=64)

            def bc64(ap16):
                return ap16.unsqueeze(2).to_broadcast([128, NH, 64])
            tt("dve", xdt[:].rearrange("p (h d) -> p h d", d=64), xc3, bc64(sm[:, SM_DT, :]), ALU.mult, ["xc_tm", "dt"], ["xdt"])
            tt("dve", xdtd[:].rearrange("p (h d) -> p h d", d=64), xc3, bc64(sm[:, SM_DTD, :]), ALU.mult, ["xc_tm", "dtd"], ["xdtd"])
            tt("pool", xd[:].rearrange("p (h d) -> p h d", d=64), xc3, bc64(dskip[:]), ALU.mult, ["xc_tm", "dskip"], ["xd"])
            bk, bkk = nbank()
            for g in range(2):
                mm(bk[:, g * 128:(g + 1) * 128], xcT[:, 8 + g, :], xcT[:, 10 + g, :], True, True, ["xcT"], [bkk])
            tt("dve", cbm[:], bk[:, 0:256].rearrange("p (g l) -> p g l", l=128), U[:].unsqueeze(1).to_broadcast([128, 2, 128]),
               ALU.mult, [bkk, "U"], ["cbm"])
            for g in range(2):
                tt("dve", rhs_all[:], U[:].unsqueeze(1).to_broadcast([128, 8, 128]),
                   sm[:, SM_DA, g * 8:(g + 1) * 8].unsqueeze(2).to_broadcast([128, 8, 128]), ALU.mult, ["U", "da"], ["rhs_all"])
                for q4 in range(2):
                    bk, bkk = nbank()
                    mm(bk[:, :], Ls[:], rhs_all[:, q4 * 4:(q4 + 1) * 4, :], True, True, ["Ls", "rhs_all"], [bkk])
                    act(expseg[:, q4 * 4:(q4 + 1) * 4, :], bk[:, :].rearrange("p (h l) -> p h l", l=128), AF.Exp, [bkk], ["expseg"])
                tt("pool", LT[:, g * 8:(g + 1) * 8, :], expseg[:], cbm[:, g, :].unsqueeze(1).to_broadcast([128, 8, 128]), ALU.mult,
                   ["expseg", "cbm"], [("LT", g)])
            for g in range(2):
                gs = slice(g * 512, (g + 1) * 512)
                bky, bkyk = nbank()
                mm(bky[:, :], idb[:], xd[:, gs], True, False, ["idb", "xd"], [bkyk])
                for r in range(8):
                    h = g * 8 + r
                    mm(bky[:, r * 64:(r + 1) * 64], LT[:, h, :], xdt[:, h * 64:(h + 1) * 64], False, r == 7,
                       [("LT", g), "xdt"], [bkyk])
                bko, bkok = nbank()
                mm(bko[:, :], xcT[:, 10 + g, :], state_bf[:, gs], True, True, ["xcT", ("state_bf", g)], [bkok])
                tt("dve", yoff_s[:, gs].rearrange("p (h d) -> p h d", d=64), bko[:, :].rearrange("p (h d) -> p h d", d=64),
                   sm[:, SM_ECUM, g * 8:(g + 1) * 8].unsqueeze(2).to_broadcast([128, 8, 64]), ALU.mult, [bkok, "ecum"], [("yoff", g)])
                tt("dve", ybuf[:, gs], bky[:, :], yoff_s[:, gs], ALU.add, [bkyk, ("yoff", g)], [("ybuf", g)])
                bks, bksk = nbank()
                mm(bks[:, :], B_tm[:, g * 128:(g + 1) * 128], xdtd[:, gs], True, True, ["B_tm", "xdtd"], [bksk])
                tt("pool", stmp[:, gs].rearrange("p (h d) -> p h d", d=64), state[:, gs].rearrange("p (h d) -> p h d", d=64),
                   sm[:, SM_EDEC, g * 8:(g + 1) * 8].unsqueeze(2).to_broadcast([128, 8, 64]), ALU.mult, [("state", g), "edec"], [("stmp", g)])
                tt("dve", state[:, gs], bks[:, :], stmp[:, gs], ALU.add, [bksk, ("stmp", g)], [("state", g)])
                cp("pool", state_bf[:, gs], state[:, gs], [("state", g)], [("state_bf", g)])
                tt("pool", y2[:, gs], ybuf[:, gs], sz[:, gs], ALU.mult, [("ybuf", g), "sz"], [("y2", g)])
                act(st["junk"][:, gs], y2[:, gs], AF.Square, [("y2", g)], ["junk", ("gssq", g)], accum_out=gst[:, g:g + 1])
                act(gst[:, 2 + g:3 + g], gst[:, g:g + 1], AF.Ln, [("gssq", g)], [("gln", g)], scale=1.0 / 512.0, bias=EPS)
                act(gst[:, 4 + g:5 + g], gst[:, 2 + g:3 + g], AF.Exp, [("gln", g)], [("grstd", g)], scale=-0.5)
                ts("dve", ybf[:, gs], y2[:, gs], gst[:, 4 + g:5 + g], None, ALU.mult, None, [("y2", g), ("grstd", g)], ["ybf"])
            bk, bkk = nbank()
            bkb = bk[:].bitcast(BF16)
            for cc in range(8):
                fw.op("pe", lambda cc=cc: nc.tensor.transpose(out=bkb[:, cc * 128:(cc + 1) * 128], in_=ybf[:, cc * 128:(cc + 1) * 128],
                                                              identity=idb[:]), ["ybf", "idb"], [bkk])
            q = b % 4
            cp("act", yT[:, :, q * 128:(q + 1) * 128], bkb[:, :].rearrange("p (c t) -> p c t", t=128), [bkk], ["yT"])
            if q == 3:
                t0 = (b - 3) * 128
                fw.dma("pool", "yTst", yT_s[:, t0:t0 + 512].rearrange("(c p) t -> p c t", p=128), yT[:], reads=["yT"], writes=[])
        fw.barrier()

    with contextlib.ExitStack() as es:
        w2 = sb(es, "w2", [128, 8, 4112], BF16)
        with contextlib.ExitStack() as es2:
            load_w_cols(es2, w2, 2576, 4112, "w2")
            fw.barrier()
        st = make_norm(es)
        gq2 = sb(es, "gq2", [128, 1], F32)
        gk2 = sb(es, "gk2", [128, 1], F32)
        fb = sb(es, "fb", [128, NH], F32)
        fw.dma("sp", "p2a", gq2[:], gq2_d[:, :], writes=["gq2"])
        fw.dma("sp", "p2b", gk2[:], gk2_d[:, :], writes=["gk2"])
        fw.dma("sp", "p2c", fb[:], fb_bc_d[:, :], writes=["fb"])
        sqb = sb(es, "sqb", [128, 512], F32)
        qst = sb(es, "qst", [128, 4, 8], F32)
        qn = [sb(es, "qn%d" % i, [128, D], BF16) for i in range(2)]
        QTc = sb(es, "QTc", [128, 8, 512], BF16)
        KTc = sb(es, "KTc", [128, 8, 512], BF16)
        zsT = sb(es, "zsT", [128, 8, 512], BF16)
        QAc = sb(es, "QAc", [NH, 512], BF16)
        v_tm = [sb(es, "v_tm%d" % i, [128, NH, 65], BF16) for i in range(2)]
        fs = sb(es, "fs", [128, 6, NH], F32)
        carry = sb(es, "carry", [128, NH], F32)
        fq8 = sb(es, "fq8", [128, NH], BF16)
        for i in range(2):
            memset("dve", v_tm[i][:], 1.0, [("v_tm", i)])
        memset("dve", carry[:], 0.0, ["carry"])
        issue_x_load(st, 0)
        if NB > 1:
            issue_x_load(st, 1)
        compute_hT(st, 0)
        for b in range(NB):
            hT = st["hT"][b % 2]
            hTk = ("hT", b % 2)
            q4 = b % 4
            for qi, (coff, gain, dstT, dkey) in enumerate(((0, gq2, QTc, "QTc"), (1024, gk2, KTc, "KTc"))):
                for g in range(2):
                    bk, bkk = nbank()
                    for kc in range(8):
                        mm(bk[:, :], hT[:, kc, :], w2[:, kc, coff + g * 512:coff + (g + 1) * 512], kc == 0, kc == 7, ["w2", hTk], [bkk])
                    act(sqb[:], bk[:, :], AF.Square, [bkk], ["sqb"])
                    fw.op("dve", lambda: nc.vector.tensor_reduce(out=qst[:, 0, :], in_=sqb[:].rearrange("p (h d) -> p h d", d=64),
                                                                 axis=AX.X, op=ALU.add), ["sqb"], ["qss"])
                    act(qst[:, 1, :], qst[:, 0, :], AF.Ln, ["qss"], ["qln"], scale=1.0 / 64.0, bias=EPS)
                    act(qst[:, 2, :], qst[:, 1, :], AF.Exp, ["qln"], ["qrs"], scale=-0.5)
                    tt("dve", qn[qi][:, g * 512:(g + 1) * 512].rearrange("p (h d) -> p h d", d=64),
                       bk[:, :].rearrange("p (h d) -> p h d", d=64), qst[:, 2, :].unsqueeze(2).to_broadcast([128, 8, 64]), ALU.mult,
                       [bkk, "qrs"], [("qn", qi)])

            def qk_transposes():
                for qi, (coff, gain, dstT, dkey) in enumerate(((0, gq2, QTc, "QTc"), (1024, gk2, KTc, "KTc"))):
                    bk, bkk = nbank()
                    bkb = bk[:].bitcast(BF16)
                    for cc in range(8):
                        fw.op("pe", lambda cc=cc, qi=qi: nc.tensor.transpose(out=bkb[:, cc * 128:(cc + 1) * 128],
                                                                             in_=qn[qi][:, cc * 128:(cc + 1) * 128], identity=idb[:]),
                              [("qn", qi), "idb"], [bkk])
                    ts("dve", dstT[:, :, q4 * 128:(q4 + 1) * 128], bkb[:, :].rearrange("p (c t) -> p c t", t=128), gain[:, 0:1], None,
                       ALU.mult, None, [bkk], [dkey])
            vs = b % 2
            for g in range(2):
                bk, bkk = nbank()
                for kc in range(8):
                    mm(bk[:, :], hT[:, kc, :], w2[:, kc, 2048 + g * 512:2048 + (g + 1) * 512], kc == 0, kc == 7, ["w2", hTk], [bkk])
                cp("act", v_tm[vs][:, g * 8:(g + 1) * 8, 0:64], bk[:, :].rearrange("p (h d) -> p h d", d=64), [bkk], [("v_tm", vs)])
            fw.dma("pool", ("vst", vs), V_s[b * 128:(b + 1) * 128, :, :], v_tm[vs][:], reads=[("v_tm", vs)], writes=[])
            for g4 in range(2):
                bk, bkk = nbank()
                for j in range(4):
                    cc = g4 * 4 + j
                    for kc in range(8):
                        mm(bk[:, j * 128:(j + 1) * 128], w2[:, kc, 3072 + cc * 128:3072 + (cc + 1) * 128], hT[:, kc, :],
                           kc == 0, kc == 7, ["w2", hTk], [bkk])
                act(zsT[:, g4 * 4:(g4 + 1) * 4, q4 * 128:(q4 + 1) * 128], bk[:, :].rearrange("p (a t) -> p a t", t=128), AF.Silu,
                    [bkk], ["zsT"])
            bk, bkk = nbank()
            for kc in range(8):
                mm(bk[:, 0:NH], hT[:, kc, :], w2[:, kc, 4096:4112], kc == 0, kc == 7, ["w2", hTk], [bkk])
            tt("dve", fs[:, 0, :], bk[:, 0:NH], fb[:], ALU.add, [bkk, "fb"], ["f0"])
            act(fs[:, 1, :], fs[:, 0, :], AF.Exp, ["f0"], ["f1"], scale=-1.0)
            act(fs[:, 2, :], fs[:, 1, :], AF.Ln, ["f1"], ["f2"], bias=1.0)
            bk, bkk = nbank()
            mm(bk[:, 0:NH], U[:], fs[:, 2, :], True, True, ["U", "f2"], [bkk])
            mm(bk[:, NH:2 * NH], ones_f[:], fs[:, 2, :], True, True, ["ones_f", "f2"], [bkk])
            if q4 == 0:
                cp("dve", refbc[:, b // 4, :], carry[:], ["carry"], ["refbc"])
            tt("dve", Ftm[:, b, :], carry[:], bk[:, 0:NH], ALU.subtract, ["carry", bkk], ["Ftm"])
            tt("dve", fs[:, 3, :], Ftm[:, b, :], refbc[:, b // 4, :], ALU.subtract, ["Ftm", "refbc"], ["f3"])
            ts("dve", fq8[:], fs[:, 3, :], 8.0, None, ALU.mult, None, ["f3"], ["fq8"])
            tt("dve", carry[:], carry[:], bk[:, NH:2 * NH], ALU.subtract, ["carry", bkk], ["carry"])
            if b + 1 < NB:
                compute_hT(st, b + 1)
            if b + 2 < NB:
                issue_x_load(st, b + 2)
            qk_transposes()
            bk, bkk = nbank()
            bkb = bk[:].bitcast(BF16)
            fw.op("pe", lambda: nc.tensor.transpose(out=bkb[0:NH, 0:128], in_=fq8[:, :], identity=idb[:]), ["fq8", "idb"], [bkk])
            cp("dve", QAc[:, q4 * 128:(q4 + 1) * 128], bkb[0:NH, 0:128], [bkk], ["QAc"])
            if q4 == 3:
                t0 = (b - 3) * 128
                fw.dma("sp", "qst", QT_s[:, t0:t0 + 512].rearrange("(c p) t -> p c t", p=128), QTc[:], reads=["QTc"], writes=[])
                fw.dma("act", "kst", KT_s[:, t0:t0 + 512].rearrange("(c p) t -> p c t", p=128), KTc[:], reads=["KTc"], writes=[])
                fw.dma("pool", "zst", zsT_s[:, t0:t0 + 512].rearrange("(c p) t -> p c t", p=128), zsT[:], reads=["zsT"], writes=[])
                fw.dma("sp", "qast", QA_s[:, t0:t0 + 512], QAc[:], reads=["QAc"], writes=[])
        fw.barrier()

    with contextlib.ExitStack() as es:
        KTh = [sb(es, "KTh%d" % i, [65, L], BF16) for i in range(2)]
        QTh = [sb(es, "QTh%d" % i, [65, L], BF16) for i in range(2)]
        Vh = [sb(es, "Vh%d" % i, [128, NB, 65], BF16) for i in range(2)]
        ZSh = [sb(es, "ZSh%d" % i, [64, L], BF16) for i in range(2)]
        NPT = 4
        PT = [sb(es, "PT%d" % i, [128, 512], BF16) for i in range(NPT)]
        btab = [sb(es, "btab%d" % i, [128, NB], F32) for i in range(2)]
        oaug = [sb(es, "oaug%d" % i, [65, 512], F32) for i in range(2)]
        sqo = [sb(es, "sqo%d" % i, [65, 512], BF16) for i in range(2)]
        r1 = sb(es, "r1", [64, 512], F32)
        r2 = sb(es, "r2", [64, 512], F32)
        t1 = sb(es, "t1", [64, 512], F32)
        oT = [sb(es, "oT%d" % i, [64, 512], BF16) for i in range(2)]
        for i in range(2):
            memset("dve", KTh[i][64:65, :], 1.0, [("KTh", i)])
        bank_pool[0] = [3, 4, 5, 6, 7]
        bank_rr[0] = 0

        def load_head(h):
            s = h % 2
            c, r0 = h // 2, (h % 2) * 64
            nq = max(1, L // 2048)
            qs = L // nq
            for i in range(nq):
                sl = slice(i * qs, (i + 1) * qs)
                fw.dma("sp", ("kld", s), KTh[s][0:64, sl], KT_s[c * 128 + r0:c * 128 + r0 + 64, sl], reads=["KT_s"], writes=[("KTh", s)])
                fw.dma("sp", ("qld", s), QTh[s][0:64, sl], QT_s[c * 128 + r0:c * 128 + r0 + 64, sl], reads=["QT_s"], writes=[("QTh", s)])
                fw.dma("sp", ("zld", s), ZSh[s][:, sl], zsT_s[h * 64:(h + 1) * 64, sl], reads=["zsT_s"], writes=[("ZSh", s)])
            fw.dma("sp", ("qald", s), QTh[s][64:65, :], QA_s[h:h + 1, :], reads=["QA_s"], writes=[("QTh", s)])
            nv = max(1, NB // 8)
            bs = NB // nv
            for i in range(nv):
                fw.dma("pool", ("vld", s), Vh[s][:, i * bs:(i + 1) * bs, :],
                       V_s[i * bs * 128:(i + 1) * bs * 128, h, :].rearrange("(b p) e -> p b e", p=128),
                       reads=["V_s"], writes=[("Vh", s)])

        def make_epilogue(h, j, os_, s):
            def epi():
                bko, bkok = banks[os_], ("bk", os_)
                cp("dve", oaug[os_][:], bko[0:65, :], [bkok], [("oaug", os_)])
                tt("pool", sqo[os_][:], oaug[os_][:], oaug[os_][:], ALU.mult, [("oaug", os_)], [("sqo", os_)])
                bkm, bkmk = banks[2], ("bk", 2)
                mm(bkm[0:64, :], wms[:], sqo[os_][:], True, True, ["wms", ("sqo", os_)], [bkmk])
                act(r1[:], bkm[0:64, :], AF.Ln, [bkmk], ["r1"])
                act(r2[:], r1[:], AF.Exp, ["r1"], ["r2"], scale=-0.5)
                tt("dve", t1[:], oaug[os_][0:64, :], r2[:], ALU.mult, [("oaug", os_), "r2"], ["t1"])
                tt("pool", oT[os_][:], t1[:], ZSh[s][:, j * 512:(j + 1) * 512], ALU.mult, ["t1", ("ZSh", s)], [("oT", os_)])
                fw.dma("act", ("ost", os_), oT_s[h * 64:(h + 1) * 64, j * 512:(j + 1) * 512], oT[os_][:], reads=[("oT", os_)], writes=[])
            return epi

        load_head(0)
        pt_rr = 0
        cnt = 0
        deferred = None
        LA = 2
        for h in range(NH):
            s = h % 2
            if deferred is not None:
                deferred()
                deferred = None
            if h + 1 < NH:
                load_head(h + 1)
            for j in range(NT):
                os_ = cnt % 2
                cnt += 1
                nblk = 4 * j + 4
                ts("dve", btab[os_][:, 0:nblk], Ftm[:, 0:nblk, h], -1.0, refbc[:, j, h:h + 1], ALU.mult, ALU.add,
                   ["Ftm", "refbc"], [("btab", os_)])
                bko, bkok = banks[os_], ("bk", os_)
                pend_pv = {}
                for t in range(nblk + LA):
                    if t < nblk:
                        i = t
                        m = i - 4 * j
                        c0 = max(m, 0) * 128
                        bk, bkk = nbank()
                        mm(bk[:, c0:512], KTh[s][0:65, i * 128:(i + 1) * 128], QTh[s][0:65, j * 512 + c0:(j + 1) * 512], True, m < 0,
                           [("KTh", s), ("QTh", s)], [bkk])
                        if m >= 0:
                            mm(bk[:, c0:c0 + 128], idb[:], negm[:], False, True, ["idb", "negm"], [bkk])
                        p = pt_rr % NPT
                        pt_rr += 1
                        act(PT[p][:, c0:512], bk[:, c0:512], AF.Exp, [bkk, ("btab", os_)], [("PT", p)], scale=0.125, bias=btab[os_][:, i:i + 1])
                        pend_pv[i] = (p, c0)
                    if t == 3 and deferred is not None:
                        deferred()
                        deferred = None
                    if t >= LA:
                        i = t - LA
                        p, c0 = pend_pv.pop(i)
                        mm(bko[0:65, c0:512], Vh[s][:, i, :], PT[p][:, c0:512], i == 0, i == nblk - 1, [("Vh", s), ("PT", p)], [bkok])
                deferred = make_epilogue(h, j, os_, s)
        if deferred is not None:
            deferred()
        bank_pool[0] = list(range(8))
        fw.barrier()

    with contextlib.ExitStack() as es:
        wo = sb(es, "wo", [128, 16, D], BF16)
        with contextlib.ExitStack() as es2:
            stg = [sb(es2, "wostg%d" % i, [128, D], F32) for i in range(2)]
            for kc in range(16):
                s = kc % 2
                fw.dma("sp" if s == 0 else "act", ("wostg", s), stg[s][:], wout_d[kc * 128:(kc + 1) * 128, :], writes=[("wostg", s)])
                gsrc = ssmg_fm if kc < 8 else attng_fm
                ts("dve", wo[:, kc, :], stg[s][:], gsrc[:, kc % 8:kc % 8 + 1], None, ALU.mult, None, [("wostg", s)], ["wo"])
            fw.barrier()
        yo = [sb(es, "yo%d" % i, [128, 16, 512], BF16) for i in range(2)]
        xr = [sb(es, "xr%d" % i, [128, D], F32) for i in range(2)]
        tmpo = [sb(es, "tmpo%d" % i, [128, D], F32) for i in range(2)]
        ob = [sb(es, "ob%d" % i, [128, D], F32) for i in range(2)]

        def load_tile(j):
            s = j % 2
            fw.dma("sp", ("yold", s), yo[s][:, 0:8, :], yT_s[:, j * 512:(j + 1) * 512].rearrange("(c p) t -> p c t", p=128),
                   reads=["yT_s"], writes=[("yo", s)])
            fw.dma("act", ("yold2", s), yo[s][:, 8:16, :], oT_s[:, j * 512:(j + 1) * 512].rearrange("(c p) t -> p c t", p=128),
                   reads=["oT_s"], writes=[("yo", s)])

        load_tile(0)
        for j in range(NT):
            s = j % 2
            if j + 1 < NT:
                load_tile(j + 1)
            for q in range(4):
                b = j * 4 + q
                xs = b % 2
                fw.dma("pool", ("xrld", xs), xr[xs][:], x_d[b * 128:(b + 1) * 128, :], writes=[("xr", xs)])
                for half in range(2):
                    hs = slice(half * 512, (half + 1) * 512)
                    bk, bkk = nbank()
                    for kc in range(16):
                        mm(bk[:, :], yo[s][:, kc, q * 128:(q + 1) * 128], wo[:, kc, hs], kc == 0, kc == 15, [("yo", s), "wo"], [bkk])
                    tt("dve", tmpo[xs][:, hs], bk[:, :], gate_bc[:, hs], ALU.mult, [bkk, "gate_bc"], [("tmpo", xs, half)])
                    tt("pool", ob[xs][:, hs], tmpo[xs][:, hs], xr[xs][:, hs], ALU.add, [("tmpo", xs, half), ("xr", xs)], [("ob", xs)])
                fw.dma("sp", ("ost3", xs), out_d[b * 128:(b + 1) * 128, :], ob[xs][:], reads=[("ob", xs)], writes=[])
        fw.barrier()
    es_all.close()
    return nc, fw


def _layout_inputs(inp, L):
    f = lambda a: np.ascontiguousarray(np.asarray(a, dtype=np.float32))
    fm8 = lambda v: f(np.asarray(v).reshape(-1, 128).T)
    bc = lambda v: f(np.broadcast_to(np.asarray(v).reshape(1, -1), (128, np.asarray(v).size)))
    shared = {
        "ng_fm": fm8(inp["norm_gain"][0]),
        "w_ada": f(inp["w_ada"][0]),
        "bada_fm": fm8(inp["b_ada"][0]),
        "bgate_bc": bc(inp["b_ada"][0][2 * D:3 * D]),
        "w_in": f(inp["w_in"][0]),
        "convw_bc": bc(np.asarray(inp["conv_w"][0]).reshape(-1)),
        "convb_row": f(np.asarray(inp["conv_b"][0]).reshape(1, -1)),
        "dtb_bc": bc(inp["dt_bias"][0]),
        "alog_bc": bc(inp["a_log"][0]),
        "dskip_bc": bc(inp["d_skip"][0]),
        "fb_bc": bc(inp["forget_bias"][0]),
        "ssmg_fm": fm8(inp["ssm_norm_gain"][0]),
        "attng_fm": fm8(inp["attn_norm_gain"][0]),
        "gq2": f(np.tile(np.asarray(inp["q_norm_gain"][0]), 2).reshape(128, 1)),
        "gk2": f(np.tile(np.asarray(inp["k_norm_gain"][0]), 2).reshape(128, 1)),
        "w_out": f(inp["w_out"][0]),
    }
    x = np.asarray(inp["x"], dtype=np.float32)
    c = np.asarray(inp["c"], dtype=np.float32)
    maps = []
    for b in range(x.shape[0]):
        m = dict(shared)
        m["x"] = np.ascontiguousarray(x[b, :L])
        m["c_fm"] = fm8(c[b])
        maps.append(m)
    return maps


def kernel(**inputs):
    x = np.asarray(inputs["x"])
    B, L, _ = x.shape
    nc, _ = build_nc(L)
    in_maps = _layout_inputs(inputs, L)
    res = run_bass_kernel_spmd(nc, in_maps, core_ids=list(range(B)))
    out = np.stack([np.asarray(r["out"], dtype=np.float32) for r in res.results], axis=0)
    return out
```

```python
import numpy as np
import concourse.bass as bass
import concourse.mybir as mybir
from concourse.bass_utils import run_bass_kernel_spmd

F32 = mybir.dt.float32
BF16 = mybir.dt.bfloat16
AF = mybir.ActivationFunctionType
ALU = mybir.AluOpType
AX = mybir.AxisListType

D = 1024
NH = 16
EPS = 1e-6
SEQ = 8192
NCORES = 8


class FW:
    def __init__(self, nc):
        self.nc = nc
        self.eng = {"pe": nc.tensor, "act": nc.scalar, "dve": nc.vector, "pool": nc.gpsimd, "sp": nc.sync}
        self.esem = {k: nc.semaphore("sem_" + k).__enter__() for k in self.eng}
        self.tick = {k: 0 for k in self.eng}
        self.waited = {}
        self.last_w = {}
        self.readers = {}
        self.dma_sems = {}
        self.n_inst = 0
        self.n_wait = 0

    def _need(self, e, sig, pend):
        sem, val, src = sig
        if src == e and e in ("pe", "sp"):
            return
        k = (e, id(sem))
        if self.waited.get(k, 0) >= val:
            return
        self.waited[k] = val
        pend[id(sem)] = (sem, val)

    def _wait(self, e, sig):
        pend = {}
        self._need(e, sig, pend)
        for sem, val in pend.values():
            self.eng[e].wait_ge(sem, val)
            self.n_wait += 1

    def _deps(self, e, reads, writes):
        pend = {}
        for k in reads:
            s = self.last_w.get(k)
            if s is not None:
                self._need(e, s, pend)
        for k in writes:
            s = self.last_w.get(k)
            if s is not None:
                self._need(e, s, pend)
            for r in self.readers.get(k, ()):
                self._need(e, r, pend)
        return list(pend.values())

    def _commit(self, sig, reads, writes):
        for k in writes:
            self.last_w[k] = sig
            self.readers[k] = []
        for k in reads:
            if k in writes:
                continue
            lst = self.readers.setdefault(k, [])
            lst[:] = [r for r in lst if r[0] is not sig[0]]
            lst.append(sig)

    def op(self, e, fn, reads=(), writes=(), inline=True):
        pend = self._deps(e, reads, writes)
        inline = inline and e != "pe" and len(pend) > 0
        for sem, val in (pend[:-1] if inline else pend):
            self.eng[e].wait_ge(sem, val)
            self.n_wait += 1
        ins = fn()
        if inline:
            ins._wait_ge(pend[-1][0], pend[-1][1])
        self.tick[e] += 1
        ins.then_inc(self.esem[e], 1)
        self._commit((self.esem[e], self.tick[e], e), reads, writes)
        self.n_inst += 1
        return ins

    def dma(self, e, semkey, out, in_, reads=(), writes=()):
        pend = self._deps(e, reads, writes)
        for sem, val in pend:
            self.eng[e].wait_ge(sem, val)
            self.n_wait += 1
        if semkey not in self.dma_sems:
            self.dma_sems[semkey] = [self.nc.semaphore("dsem%d" % len(self.dma_sems)).__enter__(), 0]
        ent = self.dma_sems[semkey]
        ins = self.eng[e].dma_start(out=out, in_=in_)
        ent[1] += 16
        ins.then_inc(ent[0], 16)
        self._commit((ent[0], ent[1], "dma"), reads, writes)
        self.n_inst += 1
        return ins

    def barrier(self):
        for e in self.eng:
            for k, ent in self.dma_sems.items():
                if ent[1] > 0:
                    self._wait(e, (ent[0], ent[1], "dma"))
            for k in self.eng:
                if self.tick[k] > 0 and k != e:
                    self._wait(e, (self.esem[k], self.tick[k], k))
        self.last_w.clear()
        self.readers.clear()


def build_nc(L, debug=False):
    NB = L // 128
    NT = L // 512
    nc = bass.Bass("TRN2", target_bir_lowering=False)
    fw = FW(nc)

    def din(name, shape, dt=F32):
        return nc.dram_tensor(name, list(shape), dt, kind="ExternalInput").ap()

    def dscr(name, shape, dt):
        return nc.dram_tensor(name, list(shape), dt, kind=("ExternalOutput" if debug else "Internal")).ap()

    x_d = din("x", [L, D])
    c_fm_d = din("c_fm", [128, 8])
    ng_fm_d = din("ng_fm", [128, 8])
    wada_d = din("w_ada", [D, 3 * D])
    bada_fm_d = din("bada_fm", [128, 24])
    bgate_bc_d = din("bgate_bc", [128, D])
    win_d = din("w_in", [D, 6688])
    convw_bc_d = din("convw_bc", [128, 4 * 1536])
    convb_row_d = din("convb_row", [1, 1536])
    dtb_bc_d = din("dtb_bc", [128, NH])
    alog_bc_d = din("alog_bc", [128, NH])
    dskip_bc_d = din("dskip_bc", [128, NH])
    fb_bc_d = din("fb_bc", [128, NH])
    ssmg_fm_d = din("ssmg_fm", [128, 8])
    attng_fm_d = din("attng_fm", [128, 8])
    gq2_d = din("gq2", [128, 1])
    gk2_d = din("gk2", [128, 1])
    wout_d = din("w_out", [2 * D, D])
    out_d = nc.dram_tensor("out", [L, D], F32, kind="ExternalOutput").ap()

    yT_s = dscr("yT_s", [D, L], BF16)
    oT_s = dscr("oT_s", [D, L], BF16)
    zsT_s = dscr("zsT_s", [D, L], BF16)
    QT_s = dscr("QT_s", [D, L], BF16)
    KT_s = dscr("KT_s", [D, L], BF16)
    QA_s = dscr("QA_s", [NH, L], BF16)
    V_s = dscr("V_s", [L, NH, 65], BF16)

    import contextlib
    es_all = contextlib.ExitStack()

    sb_cnt = [0]

    def sb(es, name, shape, dt):
        sb_cnt[0] += 1
        return es.enter_context(nc.sbuf_tensor("s%d_%s" % (sb_cnt[0], name), list(shape), dt))

    banks = [es_all.enter_context(nc.psum_tensor("bk%d" % i, [128, 512], F32)) for i in range(8)]
    bank_rr = [0]
    bank_pool = [list(range(8))]

    def nbank():
        lst = bank_pool[0]
        i = lst[bank_rr[0] % len(lst)]
        bank_rr[0] += 1
        return banks[i], ("bk", i)

    def mm(out, lhsT, rhs, start, stop, reads, writes):
        return fw.op("pe", lambda: nc.tensor.matmul(out, lhsT=lhsT, rhs=rhs, start=start, stop=stop), reads, writes)

    def act(out, in_, func, reads, writes, eng="act", **kw):
        return fw.op("act", lambda: nc.scalar.activation(out=out, in_=in_, func=func, **kw), reads, writes,
                     inline=("accum_out" not in kw))

    def tt(e, out, in0, in1, op, reads, writes):
        return fw.op(e, lambda: fw.eng[e].tensor_tensor(out=out, in0=in0, in1=in1, op=op), reads, writes)

    def ts(e, out, in0, s1, s2, op0, op1, reads, writes):
        if op1 is None:
            return fw.op(e, lambda: fw.eng[e].tensor_scalar(out=out, in0=in0, scalar1=s1, scalar2=None, op0=op0), reads, writes)
        return fw.op(e, lambda: fw.eng[e].tensor_scalar(out=out, in0=in0, scalar1=s1, scalar2=s2, op0=op0, op1=op1), reads, writes)

    def cp(e, out, in_, reads, writes):
        if e == "act":
            return act(out, in_, AF.Copy, reads, writes)
        return fw.op(e, lambda: fw.eng[e].tensor_copy(out=out, in_=in_), reads, writes)

    def memset(e, ap, val, writes):
        return fw.op(e, lambda: fw.eng[e].memset(ap, val), (), writes)

    idf = sb(es_all, "idf", [128, 128], F32)
    idb = sb(es_all, "idb", [128, 128], BF16)
    U = sb(es_all, "U", [128, 128], F32)
    Ls = sb(es_all, "Ls", [128, 128], F32)
    ones_f = sb(es_all, "ones_f", [128, 128], F32)
    negm = sb(es_all, "negm", [128, 128], BF16)
    ones_row = sb(es_all, "ones_row", [1, 128], BF16)
    gp_fm = sb(es_all, "gp_fm", [128, 8], F32)
    sh_fm = sb(es_all, "sh_fm", [128, 8], F32)
    gate_bc = sb(es_all, "gate_bc", [128, D], F32)
    Ftm = sb(es_all, "Ftm", [128, NB, NH], F32)
    refbc = sb(es_all, "refbc", [128, NT, NH], F32)
    attng_fm = sb(es_all, "attng_fm", [128, 8], F32)
    ssmg_fm = sb(es_all, "ssmg_fm", [128, 8], F32)
    wms = sb(es_all, "wms", [65, 64], BF16)

    memset("pool", idf[:], 0.0, ["idf"])
    fw.op("pool", lambda: nc.gpsimd.affine_select(out=idf[:], in_=idf[:], pattern=[[1, 128]], compare_op=ALU.not_equal,
                                                  fill=1.0, base=0, channel_multiplier=-1), ["idf"], ["idf"])
    cp("dve", idb[:], idf[:], ["idf"], ["idb"])
    memset("pool", ones_f[:], 1.0, ["ones_f"])
    fw.op("pool", lambda: nc.gpsimd.affine_select(out=U[:], in_=ones_f[:], pattern=[[1, 128]], compare_op=ALU.is_ge,
                                                  fill=0.0, base=0, channel_multiplier=-1), ["ones_f"], ["U"])
    fw.op("pool", lambda: nc.gpsimd.affine_select(out=Ls[:], in_=ones_f[:], pattern=[[-1, 128]], compare_op=ALU.is_gt,
                                                  fill=0.0, base=0, channel_multiplier=1), ["ones_f"], ["Ls"])
    ts("dve", negm[:], Ls[:], -30000.0, None, ALU.mult, None, ["Ls"], ["negm"])
    memset("dve", ones_row[:], 1.0, ["ones_row"])
    memset("dve", wms[0:64, :], 1.0 / 64.0, ["wms"])
    memset("dve", wms[64:65, :], EPS, ["wms"])
    fw.dma("sp", "c0", attng_fm[:], attng_fm_d[:, :], writes=["attng_fm"])
    fw.dma("sp", "c1", ssmg_fm[:], ssmg_fm_d[:, :], writes=["ssmg_fm"])

    with contextlib.ExitStack() as es:
        wada = sb(es, "wada", [128, 8, 3 * D], F32)
        c_fm = sb(es, "c_fm", [128, 8], F32)
        sc = sb(es, "sc", [128, 8], F32)
        scb = sb(es, "scb", [128, 8, 128], F32)
        ng_fm = sb(es, "ng_fm", [128, 8], F32)
        bada_fm = sb(es, "bada_fm", [128, 24], F32)
        bgate = sb(es, "bgate", [128, D], F32)
        modT = sb(es, "modT", [128, 16], F32)
        for kc in range(8):
            fw.dma("sp" if kc % 2 == 0 else "act", ("wada", kc), wada[:, kc, :], wada_d[kc * 128:(kc + 1) * 128, :],
                   writes=[("wada", kc)])
        fw.dma("sp", "p0a", c_fm[:], c_fm_d[:, :], writes=["c_fm"])
        fw.dma("sp", "p0b", ng_fm[:], ng_fm_d[:, :], writes=["ng_fm"])
        fw.dma("sp", "p0c", bada_fm[:], bada_fm_d[:, :], writes=["bada_fm"])
        fw.dma("sp", "p0d", bgate[:], bgate_bc_d[:, :], writes=["bgate"])
        act(sc[:], c_fm[:], AF.Silu, ["c_fm"], ["sc"])
        cp("dve", scb[:], sc[:].unsqueeze(2).to_broadcast([128, 8, 128]), ["sc"], ["scb"])
        bk, bkk = nbank()
        for n in range(16):
            for kc in range(8):
                mm(bk[:, n:n + 1], wada[:, kc, n * 128:(n + 1) * 128], sc[:, kc:kc + 1], kc == 0, kc == 7,
                   [("wada", kc), "sc"], [bkk])
        tt("dve", modT[:], bk[:, 0:16], bada_fm[:, 0:16], ALU.add, [bkk, "bada_fm"], ["modT"])
        cp("dve", sh_fm[:], modT[:, 0:8], ["modT"], ["sh_fm"])
        fw.op("dve", lambda: nc.vector.scalar_tensor_tensor(out=gp_fm[:], in0=modT[:, 8:16], scalar=1.0, in1=ng_fm[:],
                                                            op0=ALU.add, op1=ALU.mult), ["modT", "ng_fm"], ["gp_fm"])
        for half in range(2):
            bk, bkk = nbank()
            for kc in range(8):
                mm(bk[:, :], scb[:, kc, :], wada[:, kc, 2048 + half * 512:2048 + (half + 1) * 512], kc == 0, kc == 7,
                   [("wada", kc), "scb"], [bkk])
            tt("dve", gate_bc[:, half * 512:(half + 1) * 512], bk[:, :], bgate[:, half * 512:(half + 1) * 512], ALU.add,
               [bkk, "bgate"], ["gate_bc"])
        fw.barrier()

    def make_norm(es):
        st = {}
        st["xblk"] = [sb(es, "xblk%d" % i, [128, D], F32) for i in range(2)]
        st["junk"] = sb(es, "junk", [128, D], BF16)
        st["xn"] = sb(es, "xn", [128, D], BF16)
        st["stat"] = sb(es, "stat", [128, 4], F32)
        st["hT"] = [sb(es, "hT%d" % i, [128, 8, 128], BF16) for i in range(2)]
        return st

    def issue_x_load(st, b):
        slot = b % 2
        fw.dma("sp", ("xld", slot), st["xblk"][slot][:], x_d[b * 128:(b + 1) * 128, :], writes=[("xblk", slot)])

    def compute_hT(st, b):
        slot = b % 2
        xb = st["xblk"][slot]
        stat = st["stat"]
        act(st["junk"][:], xb[:], AF.Square, [("xblk", slot)], ["junk", "ssq"], accum_out=stat[:, 0:1])
        act(stat[:, 1:2], stat[:, 0:1], AF.Ln, ["ssq"], ["lnv"], scale=1.0 / D, bias=EPS)
        act(stat[:, 2:3], stat[:, 1:2], AF.Exp, ["lnv"], ["rstd"], scale=-0.5)
        ts("dve", st["xn"][:], xb[:], stat[:, 2:3], None, ALU.mult, None, [("xblk", slot), "rstd"], ["xn"])
        bk, bkk = nbank()
        bkb = bk[:].bitcast(BF16)
        for kc in range(8):
            fw.op("pe", lambda kc=kc: nc.tensor.transpose(out=bkb[:, kc * 128:(kc + 1) * 128],
                                                          in_=st["xn"][:, kc * 128:(kc + 1) * 128], identity=idb[:]),
                  ["xn", "idb"], [bkk])
        for kc in range(8):
            ts("dve", st["hT"][slot][:, kc, :], bkb[:, kc * 128:(kc + 1) * 128], gp_fm[:, kc:kc + 1], sh_fm[:, kc:kc + 1],
               ALU.mult, ALU.add, [bkk, "gp_fm", "sh_fm"], [("hT", slot)])

    def load_w_cols(es, wdst, col0, ncols, key):
        stg = [sb(es, "wstg%d" % i, [128, 2056], F32) for i in range(2)]
        n = 0
        for kc in range(8):
            c = 0
            while c < ncols:
                w = min(2056, ncols - c)
                s = n % 2
                fw.dma("sp" if s == 0 else "act", ("wstg", s), stg[s][:, 0:w],
                       win_d[kc * 128:(kc + 1) * 128, col0 + c:col0 + c + w], writes=[("wstg", s)])
                cp("pool" if s == 0 else "dve", wdst[:, kc, c:c + w], stg[s][:, 0:w], [("wstg", s)], [key])
                c += w
                n += 1

    with contextlib.ExitStack() as es:
        w1 = sb(es, "w1", [128, 8, 2576], BF16)
        dW = sb(es, "dW", [128, 4, 12, 128], BF16)
        cbrow = sb(es, "cbrow", [1, 1536], BF16)
        with contextlib.ExitStack() as es2:
            load_w_cols(es2, w1, 0, 2576, "w1")
            cwbc = sb(es2, "cwbc", [128, 4 * 1536], F32)
            cbrow_f = sb(es2, "cbrow_f", [1, 1536], F32)
            fw.dma("sp", "p1a", cwbc[:], convw_bc_d[:, :], writes=["cwbc"])
            fw.dma("sp", "p1b", cbrow_f[:], convb_row_d[:, :], writes=["cbrow_f"])
            for k in range(4):
                tt("dve", dW[:, k, :, :], idf[:].unsqueeze(1).to_broadcast([128, 12, 128]),
                   cwbc[:, k * 1536:(k + 1) * 1536].rearrange("p (c j) -> p c j", j=128), ALU.mult,
                   ["idf", "cwbc"], ["dW"])
            cp("dve", cbrow[:], cbrow_f[:], ["cbrow_f"], ["cbrow"])
            fw.barrier()
        st = make_norm(es)
        dtb = sb(es, "dtb", [128, NH], F32)
        a_bc = sb(es, "a_bc", [128, NH], F32)
        dskip = sb(es, "dskip", [128, NH], F32)
        fw.dma("sp", "p1c", dtb[:], dtb_bc_d[:, :], writes=["dtb"])
        fw.dma("sp", "p1d", a_bc[:], alog_bc_d[:, :], writes=["a_bc"])
        fw.dma("sp", "p1e", dskip[:], dskip_bc_d[:, :], writes=["dskip"])
        act(a_bc[:], a_bc[:], AF.Exp, ["a_bc"], ["a_bc"])
        ts("dve", a_bc[:], a_bc[:], -1.0, None, ALU.mult, None, ["a_bc"], ["a_bc"])
        uT = sb(es, "uT", [128, 12, 131], BF16)
        xcT = sb(es, "xcT", [128, 12, 128], BF16)
        sz = sb(es, "sz", [128, D], BF16)
        sm = sb(es, "sm", [128, 10, NH], F32)
        xc_tm = sb(es, "xc_tm", [128, D], BF16)
        B_tm = sb(es, "B_tm", [128, 256], BF16)
        xdt = sb(es, "xdt", [128, D], BF16)
        xdtd = sb(es, "xdtd", [128, D], BF16)
        xd = sb(es, "xd", [128, D], BF16)
        cbm = sb(es, "cbm", [128, 2, 128], BF16)
        rhs_all = sb(es, "rhs_all", [128, 8, 128], F32)
        expseg = sb(es, "expseg", [128, 8, 128], BF16)
        LT = sb(es, "LT", [128, NH, 128], BF16)
        yoff_s = sb(es, "yoff_s", [128, D], F32)
        ybuf = sb(es, "ybuf", [128, D], F32)
        y2 = sb(es, "y2", [128, D], F32)
        ybf = sb(es, "ybf", [128, D], BF16)
        state = sb(es, "state", [128, D], F32)
        state_bf = sb(es, "state_bf", [128, D], BF16)
        stmp = sb(es, "stmp", [128, D], F32)
        gst = sb(es, "gst", [128, 8], F32)
        yT = sb(es, "yT", [128, 8, 512], BF16)
        memset("dve", uT[:], 0.0, ["uT"])
        memset("dve", state[:], 0.0, ["state"])
        memset("pool", state_bf[:], 0.0, ["state_bf"])
        SM_DTR, SM_DT, SM_DA, SM_CUM, SM_D, SM_DTE, SM_DTD, SM_E, SM_ECUM, SM_EDEC = range(10)
        xcT2 = [xcT, sb(es, "xcT_b", [128, 12, 128], BF16)]
        sz2 = [sz, sb(es, "sz_b", [128, D], BF16)]
        dtr2 = [sb(es, "dtr%d" % i, [128, NH], F32) for i in range(2)]

        def bc64(ap16):
            return ap16.unsqueeze(2).to_broadcast([128, NH, 64])

        def stageA(b):
            sl = b % 2
            compute_hT(st, b)
            if b + 2 < NB:
                issue_x_load(st, b + 2)
            yield
            hT = st["hT"][sl]
            hTk = ("hT", sl)
            xcT_, sz_ = xcT2[sl], sz2[sl]
            for g4 in range(3):
                bk, bkk = nbank()
                for j in range(4):
                    cc = g4 * 4 + j
                    for kc in range(8):
                        mm(bk[:, j * 128:(j + 1) * 128], w1[:, kc, 1024 + cc * 128:1024 + (cc + 1) * 128], hT[:, kc, :],
                           kc == 0, kc == 7, ["w1", hTk], [bkk])
                cp("act", uT[:, g4 * 4:(g4 + 1) * 4, 3:131], bk[:, :].rearrange("p (a t) -> p a t", t=128), [bkk], ["uT"])
                yield
            for g in range(2):
                bk, bkk = nbank()
                for kc in range(8):
                    mm(bk[:, :], hT[:, kc, :], w1[:, kc, g * 512:(g + 1) * 512], kc == 0, kc == 7, ["w1", hTk], [bkk])
                act(sz_[:, g * 512:(g + 1) * 512], bk[:, :], AF.Silu, [bkk], [("sz", sl)])
                yield
            bk, bkk = nbank()
            for kc in range(8):
                mm(bk[:, 0:NH], hT[:, kc, :], w1[:, kc, 2560:2576], kc == 0, kc == 7, ["w1", hTk], [bkk])
            tt("dve", dtr2[sl][:], bk[:, 0:NH], dtb[:], ALU.add, [bkk, "dtb"], [("dtr", sl)])
            yield
            for g4 in range(3):
                bk, bkk = nbank()
                for j in range(4):
                    cc = g4 * 4 + j
                    for k in range(4):
                        mm(bk[:, j * 128:(j + 1) * 128], dW[:, k, cc, :], uT[:, cc, k:k + 128], k == 0, False,
                           ["dW", "uT"], [bkk])
                    mm(bk[:, j * 128:(j + 1) * 128], cbrow[0:1, cc * 128:(cc + 1) * 128], ones_row[0:1, :], False, True,
                       ["cbrow", "ones_row"], [bkk])
                act(xcT_[:, g4 * 4:(g4 + 1) * 4, :], bk[:, :].rearrange("p (a t) -> p a t", t=128), AF.Silu, [bkk], [("xcT", sl)])
                yield
            cp("dve", uT[:, :, 0:3], uT[:, :, 128:131], ["uT"], ["uT"])
            yield

        def stageB(b):
            sl = b % 2
            xcT_, sz_ = xcT2[sl], sz2[sl]
            xk, szk = ("xcT", sl), ("sz", sl)
            act(sm[:, SM_E, :], dtr2[sl][:], AF.Exp, [("dtr", sl)], ["dte_e"])
            act(sm[:, SM_DT, :], sm[:, SM_E, :], AF.Ln, ["dte_e"], ["dt"], bias=1.0)
            tt("dve", sm[:, SM_DA, :], sm[:, SM_DT, :], a_bc[:], ALU.mult, ["dt", "a_bc"], ["da"])
            yield
            bk, bkk = nbank()
            mm(bk[:, 0:NH], U[:], sm[:, SM_DA, :], True, True, ["U", "da"], [bkk])
            mm(bk[:, NH:2 * NH], ones_f[:], sm[:, SM_DA, :], True, True, ["ones_f", "da"], [bkk])
            cp("dve", sm[:, SM_CUM, :], bk[:, 0:NH], [bkk], ["cum"])
            tt("dve", sm[:, SM_D, :], bk[:, NH:2 * NH], sm[:, SM_CUM, :], ALU.subtract, [bkk, "cum"], ["dd"])
            act(sm[:, SM_DTE, :], sm[:, SM_D, :], AF.Exp, ["dd"], ["dte"])
            act(sm[:, SM_EDEC, :], bk[:, NH:2 * NH], AF.Exp, [bkk], ["edec"])
            act(sm[:, SM_ECUM, :], sm[:, SM_CUM, :], AF.Exp, ["cum"], ["ecum"])
            tt("dve", sm[:, SM_DTD, :], sm[:, SM_DT, :], sm[:, SM_DTE, :], ALU.mult, ["dt", "dte"], ["dtd"])
            yield
            bk, bkk = nbank()
            bkb = bk[:].bitcast(BF16)
            for cc in range(8):
                fw.op("pe", lambda cc=cc: nc.tensor.transpose(out=bkb[:, cc * 128:(cc + 1) * 128], in_=xcT_[:, cc, :],
                                                              identity=idb[:]), [xk, "idb"], [bkk])
            cp("act", xc_tm[:], bkb[:, :], [bkk], ["xc_tm"])
            yield
            bk, bkk = nbank()
            bkb = bk[:].bitcast(BF16)
            for g in range(2):
                fw.op("pe", lambda g=g: nc.tensor.transpose(out=bkb[:, g * 128:(g + 1) * 128], in_=xcT_[:, 8 + g, :],
                                                            identity=idb[:]), [xk, "idb"], [bkk])
            cp("act", B_tm[:], bkb[:, 0:256], [bkk], ["B_tm"])
            yield
            xc3 = xc_tm[:].rearrange("p (h d) -> p h d", d=64)
            tt("dve", xdt[:].rearrange("p (h d) -> p h d", d=64), xc3, bc64(sm[:, SM_DT, :]), ALU.mult, ["xc_tm", "dt"], ["xdt"])
            tt("dve", xdtd[:].rearrange("p (h d) -> p h d", d=64), xc3, bc64(sm[:, SM_DTD, :]), ALU.mult, ["xc_tm", "dtd"], ["xdtd"])
            tt("pool", xd[:].rearrange("p (h d) -> p h d", d=64), xc3, bc64(dskip[:]), ALU.mult, ["xc_tm", "dskip"], ["xd"])
            yield
            bk, bkk = nbank()
            for g in range(2):
                mm(bk[:, g * 128:(g + 1) * 128], xcT_[:, 8 + g, :], xcT_[:, 10 + g, :], True, True, [xk], [bkk])
            tt("dve", cbm[:], bk[:, 0:256].rearrange("p (g l) -> p g l", l=128), U[:].unsqueeze(1).to_broadcast([128, 2, 128]),
               ALU.mult, [bkk, "U"], ["cbm"])
            yield
            for g in range(2):
                tt("dve", rhs_all[:], U[:].unsqueeze(1).to_broadcast([128, 8, 128]),
                   sm[:, SM_DA, g * 8:(g + 1) * 8].unsqueeze(2).to_broadcast([128, 8, 128]), ALU.mult, ["U", "da"], ["rhs_all"])
                for q4 in range(2):
                    bk, bkk = nbank()
                    mm(bk[:, :], Ls[:], rhs_all[:, q4 * 4:(q4 + 1) * 4, :], True, True, ["Ls", "rhs_all"], [bkk])
                    act(expseg[:, q4 * 4:(q4 + 1) * 4, :], bk[:, :].rearrange("p (h l) -> p h l", l=128), AF.Exp, [bkk], ["expseg"])
                    yield
                tt("pool", LT[:, g * 8:(g + 1) * 8, :], expseg[:], cbm[:, g, :].unsqueeze(1).to_broadcast([128, 8, 128]), ALU.mult,
                   ["expseg", "cbm"], [("LT", g)])
                yield
            for g in range(2):
                gs = slice(g * 512, (g + 1) * 512)
                bky, bkyk = nbank()
                mm(bky[:, :], idb[:], xd[:, gs], True, False, ["idb", "xd"], [bkyk])
                for r in range(8):
                    h = g * 8 + r
                    mm(bky[:, r * 64:(r + 1) * 64], LT[:, h, :], xdt[:, h * 64:(h + 1) * 64], False, r == 7,
                       [("LT", g), "xdt"], [bkyk])
                bko, bkok = nbank()
                mm(bko[:, :], xcT_[:, 10 + g, :], state_bf[:, gs], True, True, [xk, ("state_bf", g)], [bkok])
                tt("dve", yoff_s[:, gs].rearrange("p (h d) -> p h d", d=64), bko[:, :].rearrange("p (h d) -> p h d", d=64),
                   sm[:, SM_ECUM, g * 8:(g + 1) * 8].unsqueeze(2).to_broadcast([128, 8, 64]), ALU.mult, [bkok, "ecum"], [("yoff", g)])
                tt("dve", ybuf[:, gs], bky[:, :], yoff_s[:, gs], ALU.add, [bkyk, ("yoff", g)], [("ybuf", g)])
                yield
                bks, bksk = nbank()
                mm(bks[:, :], B_tm[:, g * 128:(g + 1) * 128], xdtd[:, gs], True, True, ["B_tm", "xdtd"], [bksk])
                tt("pool", stmp[:, gs].rearrange("p (h d) -> p h d", d=64), state[:, gs].rearrange("p (h d) -> p h d", d=64),
                   sm[:, SM_EDEC, g * 8:(g + 1) * 8].unsqueeze(2).to_broadcast([128, 8, 64]), ALU.mult, [("state", g), "edec"], [("stmp", g)])
                tt("dve", state[:, gs], bks[:, :], stmp[:, gs], ALU.add, [bksk, ("stmp", g)], [("state", g)])
                cp("pool", state_bf[:, gs], state[:, gs], [("state", g)], [("state_bf", g)])
                yield
                tt("pool", y2[:, gs], ybuf[:, gs], sz_[:, gs], ALU.mult, [("ybuf", g), szk], [("y2", g)])
                act(st["junk"][:, gs], y2[:, gs], AF.Square, [("y2", g)], ["junk", ("gssq", g)], accum_out=gst[:, g:g + 1])
                act(gst[:, 2 + g:3 + g], gst[:, g:g + 1], AF.Ln, [("gssq", g)], [("gln", g)], scale=1.0 / 512.0, bias=EPS)
                act(gst[:, 4 + g:5 + g], gst[:, 2 + g:3 + g], AF.Exp, [("gln", g)], [("grstd", g)], scale=-0.5)
                ts("dve", ybf[:, gs], y2[:, gs], gst[:, 4 + g:5 + g], None, ALU.mult, None, [("y2", g), ("grstd", g)], ["ybf"])
                yield
            bk, bkk = nbank()
            bkb = bk[:].bitcast(BF16)
            for cc in range(8):
                fw.op("pe", lambda cc=cc: nc.tensor.transpose(out=bkb[:, cc * 128:(cc + 1) * 128], in_=ybf[:, cc * 128:(cc + 1) * 128],
                                                              identity=idb[:]), ["ybf", "idb"], [bkk])
            q = b % 4
            cp("act", yT[:, :, q * 128:(q + 1) * 128], bkb[:, :].rearrange("p (c t) -> p c t", t=128), [bkk], ["yT"])
            if q == 3:
                t0 = (b - 3) * 128
                fw.dma("pool", "yTst", yT_s[:, t0:t0 + 512].rearrange("(c p) t -> p c t", p=128), yT[:], reads=["yT"], writes=[])
            yield

        def interleave(ga, gb, ra=1, rb=1):
            alive_a, alive_b = ga is not None, gb is not None
            while alive_a or alive_b:
                for _ in range(ra):
                    if alive_a:
                        try:
                            next(ga)
                        except StopIteration:
                            alive_a = False
                for _ in range(rb):
                    if alive_b:
                        try:
                            next(gb)
                        except StopIteration:
                            alive_b = False

        issue_x_load(st, 0)
        if NB > 1:
            issue_x_load(st, 1)
        interleave(stageA(0), None)
        for b in range(NB):
            interleave(stageB(b), stageA(b + 1) if b + 1 < NB else None, 2, 1)
        fw.barrier()

    with contextlib.ExitStack() as es:
        w2 = sb(es, "w2", [128, 8, 4112], BF16)
        with contextlib.ExitStack() as es2:
            load_w_cols(es2, w2, 2576, 4112, "w2")
            fw.barrier()
        st = make_norm(es)
        gq2 = sb(es, "gq2", [128, 1], F32)
        gk2 = sb(es, "gk2", [128, 1], F32)
        fb = sb(es, "fb", [128, NH], F32)
        fw.dma("sp", "p2a", gq2[:], gq2_d[:, :], writes=["gq2"])
        fw.dma("sp", "p2b", gk2[:], gk2_d[:, :], writes=["gk2"])
        fw.dma("sp", "p2c", fb[:], fb_bc_d[:, :], writes=["fb"])
        sqb = sb(es, "sqb", [128, 512], F32)
        qst = sb(es, "qst", [128, 4, 8], F32)
        qn = [sb(es, "qn%d" % i, [128, D], BF16) for i in range(2)]
        QTc = sb(es, "QTc", [128, 8, 512], BF16)
        KTc = sb(es, "KTc", [128, 8, 512], BF16)
        zsT = sb(es, "zsT", [128, 8, 512], BF16)
        QAc = sb(es, "QAc", [NH, 512], BF16)
        v_tm = [sb(es, "v_tm%d" % i, [128, NH, 65], BF16) for i in range(2)]
        fs = sb(es, "fs", [128, 6, NH], F32)
        carry = sb(es, "carry", [128, NH], F32)
        fq8 = sb(es, "fq8", [128, NH], BF16)
        for i in range(2):
            memset("dve", v_tm[i][:], 1.0, [("v_tm", i)])
        memset("dve", carry[:], 0.0, ["carry"])
        issue_x_load(st, 0)
        if NB > 1:
            issue_x_load(st, 1)
        compute_hT(st, 0)
        for b in range(NB):
            hT = st["hT"][b % 2]
            hTk = ("hT", b % 2)
            q4 = b % 4
            for qi, (coff, gain, dstT, dkey) in enumerate(((0, gq2, QTc, "QTc"), (1024, gk2, KTc, "KTc"))):
                for g in range(2):
                    bk, bkk = nbank()
                    for kc in range(8):
                        mm(bk[:, :], hT[:, kc, :], w2[:, kc, coff + g * 512:coff + (g + 1) * 512], kc == 0, kc == 7, ["w2", hTk], [bkk])
                    act(sqb[:], bk[:, :], AF.Square, [bkk], ["sqb"])
                    fw.op("dve", lambda: nc.vector.tensor_reduce(out=qst[:, 0, :], in_=sqb[:].rearrange("p (h d) -> p h d", d=64),
                                                                 axis=AX.X, op=ALU.add), ["sqb"], ["qss"])
                    act(qst[:, 1, :], qst[:, 0, :], AF.Ln, ["qss"], ["qln"], scale=1.0 / 64.0, bias=EPS)
                    act(qst[:, 2, :], qst[:, 1, :], AF.Exp, ["qln"], ["qrs"], scale=-0.5)
                    tt("dve", qn[qi][:, g * 512:(g + 1) * 512].rearrange("p (h d) -> p h d", d=64),
                       bk[:, :].rearrange("p (h d) -> p h d", d=64), qst[:, 2, :].unsqueeze(2).to_broadcast([128, 8, 64]), ALU.mult,
                       [bkk, "qrs"], [("qn", qi)])

            def qk_transposes():
                for qi, (coff, gain, dstT, dkey) in enumerate(((0, gq2, QTc, "QTc"), (1024, gk2, KTc, "KTc"))):
                    bk, bkk = nbank()
                    bkb = bk[:].bitcast(BF16)
                    for cc in range(8):
                        fw.op("pe", lambda cc=cc, qi=qi: nc.tensor.transpose(out=bkb[:, cc * 128:(cc + 1) * 128],
                                                                             in_=qn[qi][:, cc * 128:(cc + 1) * 128], identity=idb[:]),
                              [("qn", qi), "idb"], [bkk])
                    ts("dve", dstT[:, :, q4 * 128:(q4 + 1) * 128], bkb[:, :].rearrange("p (c t) -> p c t", t=128), gain[:, 0:1], None,
                       ALU.mult, None, [bkk], [dkey])
            vs = b % 2
            for g in range(2):
                bk, bkk = nbank()
                for kc in range(8):
                    mm(bk[:, :], hT[:, kc, :], w2[:, kc, 2048 + g * 512:2048 + (g + 1) * 512], kc == 0, kc == 7, ["w2", hTk], [bkk])
                cp("act", v_tm[vs][:, g * 8:(g + 1) * 8, 0:64], bk[:, :].rearrange("p (h d) -> p h d", d=64), [bkk], [("v_tm", vs)])
            fw.dma("pool", ("vst", vs), V_s[b * 128:(b + 1) * 128, :, :], v_tm[vs][:], reads=[("v_tm", vs)], writes=[])
            for g4 in range(2):
                bk, bkk = nbank()
                for j in range(4):
                    cc = g4 * 4 + j
                    for kc in range(8):
                        mm(bk[:, j * 128:(j + 1) * 128], w2[:, kc, 3072 + cc * 128:3072 + (cc + 1) * 128], hT[:, kc, :],
                           kc == 0, kc == 7, ["w2", hTk], [bkk])
                act(zsT[:, g4 * 4:(g4 + 1) * 4, q4 * 128:(q4 + 1) * 128], bk[:, :].rearrange("p (a t) -> p a t", t=128), AF.Silu,
                    [bkk], ["zsT"])
            bk, bkk = nbank()
            for kc in range(8):
                mm(bk[:, 0:NH], hT[:, kc, :], w2[:, kc, 4096:4112], kc == 0, kc == 7, ["w2", hTk], [bkk])
            tt("dve", fs[:, 0, :], bk[:, 0:NH], fb[:], ALU.add, [bkk, "fb"], ["f0"])
            act(fs[:, 1, :], fs[:, 0, :], AF.Exp, ["f0"], ["f1"], scale=-1.0)
            act(fs[:, 2, :], fs[:, 1, :], AF.Ln, ["f1"], ["f2"], bias=1.0)
            bk, bkk = nbank()
            mm(bk[:, 0:NH], U[:], fs[:, 2, :], True, True, ["U", "f2"], [bkk])
            mm(bk[:, NH:2 * NH], ones_f[:], fs[:, 2, :], True, True, ["ones_f", "f2"], [bkk])
            if q4 == 0:
                cp("dve", refbc[:, b // 4, :], carry[:], ["carry"], ["refbc"])
            tt("dve", Ftm[:, b, :], carry[:], bk[:, 0:NH], ALU.subtract, ["carry", bkk], ["Ftm"])
            tt("dve", fs[:, 3, :], Ftm[:, b, :], refbc[:, b // 4, :], ALU.subtract, ["Ftm", "refbc"], ["f3"])
            ts("dve", fq8[:], fs[:, 3, :], 8.0, None, ALU.mult, None, ["f3"], ["fq8"])
            tt("dve", carry[:], carry[:], bk[:, NH:2 * NH], ALU.subtract, ["carry", bkk], ["carry"])
            if b + 1 < NB:
                compute_hT(st, b + 1)
            if b + 2 < NB:
                issue_x_load(st, b + 2)
            qk_transposes()
            bk, bkk = nbank()
            bkb = bk[:].bitcast(BF16)
            fw.op("pe", lambda: nc.tensor.transpose(out=bkb[0:NH, 0:128], in_=fq8[:, :], identity=idb[:]), ["fq8", "idb"], [bkk])
            cp("dve", QAc[:, q4 * 128:(q4 + 1) * 128], bkb[0:NH, 0:128], [bkk], ["QAc"])
            if q4 == 3:
                t0 = (b - 3) * 128
                fw.dma("sp", "qst", QT_s[:, t0:t0 + 512].rearrange("(c p) t -> p c t", p=128), QTc[:], reads=["QTc"], writes=[])
                fw.dma("act", "kst", KT_s[:, t0:t0 + 512].rearrange("(c p) t -> p c t", p=128), KTc[:], reads=["KTc"], writes=[])
                fw.dma("pool", "zst", zsT_s[:, t0:t0 + 512].rearrange("(c p) t -> p c t", p=128), zsT[:], reads=["zsT"], writes=[])
                fw.dma("sp", "qast", QA_s[:, t0:t0 + 512], QAc[:], reads=["QAc"], writes=[])
        fw.barrier()

    with contextlib.ExitStack() as es:
        KTh = [sb(es, "KTh%d" % i, [65, L], BF16) for i in range(2)]
        QTh = [sb(es, "QTh%d" % i, [65, L], BF16) for i in range(2)]
        Vh = [sb(es, "Vh%d" % i, [128, NB, 65], BF16) for i in range(2)]
        ZSh = [sb(es, "ZSh%d" % i, [64, L], BF16) for i in range(2)]
        NPT = 4
        PT = [sb(es, "PT%d" % i, [128, 512], BF16) for i in range(NPT)]
        btab = [sb(es, "btab%d" % i, [128, NB], F32) for i in range(2)]
        oaug = [sb(es, "oaug%d" % i, [65, 512], F32) for i in range(2)]
        sqo = [sb(es, "sqo%d" % i, [65, 512], BF16) for i in range(2)]
        r1 = sb(es, "r1", [64, 512], F32)
        r2 = sb(es, "r2", [64, 512], F32)
        t1 = sb(es, "t1", [64, 512], F32)
        oT = [sb(es, "oT%d" % i, [64, 512], BF16) for i in range(2)]
        for i in range(2):
            memset("dve", KTh[i][64:65, :], 1.0, [("KTh", i)])
        bank_pool[0] = [3, 4, 5, 6, 7]
        bank_rr[0] = 0

        def load_head(h):
            s = h % 2
            c, r0 = h // 2, (h % 2) * 64
            nq = max(1, L // 2048)
            qs = L // nq
            for i in range(nq):
                sl = slice(i * qs, (i + 1) * qs)
                fw.dma("sp", ("kld", s), KTh[s][0:64, sl], KT_s[c * 128 + r0:c * 128 + r0 + 64, sl], reads=["KT_s"], writes=[("KTh", s)])
                fw.dma("sp", ("qld", s), QTh[s][0:64, sl], QT_s[c * 128 + r0:c * 128 + r0 + 64, sl], reads=["QT_s"], writes=[("QTh", s)])
                fw.dma("sp", ("zld", s), ZSh[s][:, sl], zsT_s[h * 64:(h + 1) * 64, sl], reads=["zsT_s"], writes=[("ZSh", s)])
            fw.dma("sp", ("qald", s), QTh[s][64:65, :], QA_s[h:h + 1, :], reads=["QA_s"], writes=[("QTh", s)])
            nv = max(1, NB // 8)
            bs = NB // nv
            for i in range(nv):
                fw.dma("pool", ("vld", s), Vh[s][:, i * bs:(i + 1) * bs, :],
                       V_s[i * bs * 128:(i + 1) * bs * 128, h, :].rearrange("(b p) e -> p b e", p=128),
                       reads=["V_s"], writes=[("Vh", s)])

        def make_epilogue(h, j, os_, s):
            def epi():
                bko, bkok = banks[os_], ("bk", os_)
                cp("dve", oaug[os_][:], bko[0:65, :], [bkok], [("oaug", os_)])
                tt("pool", sqo[os_][:], oaug[os_][:], oaug[os_][:], ALU.mult, [("oaug", os_)], [("sqo", os_)])
                bkm, bkmk = banks[2], ("bk", 2)
                mm(bkm[0:64, :], wms[:], sqo[os_][:], True, True, ["wms", ("sqo", os_)], [bkmk])
                act(r1[:], bkm[0:64, :], AF.Ln, [bkmk], ["r1"])
                act(r2[:], r1[:], AF.Exp, ["r1"], ["r2"], scale=-0.5)
                tt("dve", t1[:], oaug[os_][0:64, :], r2[:], ALU.mult, [("oaug", os_), "r2"], ["t1"])
                tt("pool", oT[os_][:], t1[:], ZSh[s][:, j * 512:(j + 1) * 512], ALU.mult, ["t1", ("ZSh", s)], [("oT", os_)])
                fw.dma("act", ("ost", os_), oT_s[h * 64:(h + 1) * 64, j * 512:(j + 1) * 512], oT[os_][:], reads=[("oT", os_)], writes=[])
            return epi

        load_head(0)
        pt_rr = 0
        cnt = 0
        deferred = None
        LA = 2
        for h in range(NH):
            s = h % 2
            if deferred is not None:
                deferred()
                deferred = None
            if h + 1 < NH:
                load_head(h + 1)
            for j in range(NT):
                os_ = cnt % 2
                cnt += 1
                nblk = 4 * j + 4
                ts("dve", btab[os_][:, 0:nblk], Ftm[:, 0:nblk, h], -1.0, refbc[:, j, h:h + 1], ALU.mult, ALU.add,
                   ["Ftm", "refbc"], [("btab", os_)])
                bko, bkok = banks[os_], ("bk", os_)
                pend_pv = {}
                for t in range(nblk + LA):
                    if t < nblk:
                        i = t
                        m = i - 4 * j
                        c0 = max(m, 0) * 128
                        bk, bkk = nbank()
                        mm(bk[:, c0:512], KTh[s][0:65, i * 128:(i + 1) * 128], QTh[s][0:65, j * 512 + c0:(j + 1) * 512], True, m < 0,
                           [("KTh", s), ("QTh", s)], [bkk])
                        if m >= 0:
                            mm(bk[:, c0:c0 + 128], idb[:], negm[:], False, True, ["idb", "negm"], [bkk])
                        p = pt_rr % NPT
                        pt_rr += 1
                        act(PT[p][:, c0:512], bk[:, c0:512], AF.Exp, [bkk, ("btab", os_)], [("PT", p)], scale=0.125, bias=btab[os_][:, i:i + 1])
                        pend_pv[i] = (p, c0)
                    if t == 3 and deferred is not None:
                        deferred()
                        deferred = None
                    if t >= LA:
                        i = t - LA
                        p, c0 = pend_pv.pop(i)
                        mm(bko[0:65, c0:512], Vh[s][:, i, :], PT[p][:, c0:512], i == 0, i == nblk - 1, [("Vh", s), ("PT", p)], [bkok])
                deferred = make_epilogue(h, j, os_, s)
        if deferred is not None:
            deferred()
        bank_pool[0] = list(range(8))
        fw.barrier()

    with contextlib.ExitStack() as es:
        wo = sb(es, "wo", [128, 16, D], BF16)
        with contextlib.ExitStack() as es2:
            stg = [sb(es2, "wostg%d" % i, [128, D], F32) for i in range(2)]
            for kc in range(16):
                s = kc % 2
                fw.dma("sp" if s == 0 else "act", ("wostg", s), stg[s][:], wout_d[kc * 128:(kc + 1) * 128, :], writes=[("wostg", s)])
                gsrc = ssmg_fm if kc < 8 else attng_fm
                ts("dve", wo[:, kc, :], stg[s][:], gsrc[:, kc % 8:kc % 8 + 1], None, ALU.mult, None, [("wostg", s)], ["wo"])
            fw.barrier()
        yo = [sb(es, "yo%d" % i, [128, 16, 512], BF16) for i in range(2)]
        xr = [sb(es, "xr%d" % i, [128, D], F32) for i in range(2)]
        tmpo = [sb(es, "tmpo%d" % i, [128, D], F32) for i in range(2)]
        ob = [sb(es, "ob%d" % i, [128, D], F32) for i in range(2)]

        def load_tile(j):
            s = j % 2
            fw.dma("sp", ("yold", s), yo[s][:, 0:8, :], yT_s[:, j * 512:(j + 1) * 512].rearrange("(c p) t -> p c t", p=128),
                   reads=["yT_s"], writes=[("yo", s)])
            fw.dma("act", ("yold2", s), yo[s][:, 8:16, :], oT_s[:, j * 512:(j + 1) * 512].rearrange("(c p) t -> p c t", p=128),
                   reads=["oT_s"], writes=[("yo", s)])

        load_tile(0)
        for j in range(NT):
            s = j % 2
            if j + 1 < NT:
                load_tile(j + 1)
            for q in range(4):
                b = j * 4 + q
                xs = b % 2
                fw.dma("pool", ("xrld", xs), xr[xs][:], x_d[b * 128:(b + 1) * 128, :], writes=[("xr", xs)])
                for half in range(2):
                    hs = slice(half * 512, (half + 1) * 512)
                    bk, bkk = nbank()
                    for kc in range(16):
                        mm(bk[:, :], yo[s][:, kc, q * 128:(q + 1) * 128], wo[:, kc, hs], kc == 0, kc == 15, [("yo", s), "wo"], [bkk])
                    tt("dve", tmpo[xs][:, hs], bk[:, :], gate_bc[:, hs], ALU.mult, [bkk, "gate_bc"], [("tmpo", xs, half)])
                    tt("pool", ob[xs][:, hs], tmpo[xs][:, hs], xr[xs][:, hs], ALU.add, [("tmpo", xs, half), ("xr", xs)], [("ob", xs)])
                fw.dma("sp", ("ost3", xs), out_d[b * 128:(b + 1) * 128, :], ob[xs][:], reads=[("ob", xs)], writes=[])
        fw.barrier()
    es_all.close()
    return nc, fw


def _layout_inputs(inp, L):
    f = lambda a: np.ascontiguousarray(np.asarray(a, dtype=np.float32))
    fm8 = lambda v: f(np.asarray(v).reshape(-1, 128).T)
    bc = lambda v: f(np.broadcast_to(np.asarray(v).reshape(1, -1), (128, np.asarray(v).size)))
    shared = {
        "ng_fm": fm8(inp["norm_gain"][0]),
        "w_ada": f(inp["w_ada"][0]),
        "bada_fm": fm8(inp["b_ada"][0]),
        "bgate_bc": bc(inp["b_ada"][0][2 * D:3 * D]),
        "w_in": f(inp["w_in"][0]),
        "convw_bc": bc(np.asarray(inp["conv_w"][0]).reshape(-1)),
        "convb_row": f(np.asarray(inp["conv_b"][0]).reshape(1, -1)),
        "dtb_bc": bc(inp["dt_bias"][0]),
        "alog_bc": bc(inp["a_log"][0]),
        "dskip_bc": bc(inp["d_skip"][0]),
        "fb_bc": bc(inp["forget_bias"][0]),
        "ssmg_fm": fm8(inp["ssm_norm_gain"][0]),
        "attng_fm": fm8(inp["attn_norm_gain"][0]),
        "gq2": f(np.tile(np.asarray(inp["q_norm_gain"][0]), 2).reshape(128, 1)),
        "gk2": f(np.tile(np.asarray(inp["k_norm_gain"][0]), 2).reshape(128, 1)),
        "w_out": f(inp["w_out"][0]),
    }
    x = np.asarray(inp["x"], dtype=np.float32)
    c = np.asarray(inp["c"], dtype=np.float32)
    maps = []
    for b in range(x.shape[0]):
        m = dict(shared)
        m["x"] = np.ascontiguousarray(x[b, :L])
        m["c_fm"] = fm8(c[b])
        maps.append(m)
    return maps


def kernel(**inputs):
    x = np.asarray(inputs["x"])
    B, L, _ = x.shape
    nc, _ = build_nc(L)
    in_maps = _layout_inputs(inputs, L)
    res = run_bass_kernel_spmd(nc, in_maps, core_ids=list(range(B)))
    out = np.stack([np.asarray(r["out"], dtype=np.float32) for r in res.results], axis=0)
    return out
```
